# Optimizing a Trainium2 kernel written in Bass

```python
import math
import jax, jax.numpy as jnp
from jax import lax
import numpy as np

D_MODEL = 1024
BATCH = 8
SEQ = 2048
DEPTH = 1
DEC_BATCH = 128
DEC_SEQ = 8
PAST_LEN = 2048
PAGE_SIZE = 128

N_META = 16
GDN_HEADS = 8
GDN_DK = 128
GDN_DV = 128
CONV_W = 4
GDN_CHUNK = 64
ATT_HEADS = 8
ATT_KV_HEADS = 2
ATT_DH = 128
IDX_HEADS = 16
IDX_DIM = 64
TOPK_MAX = 256
Q_BLOCK = 128
PEER_HEADS = 8
PEER_NKEYS = 128
PEER_EXPERTS = PEER_NKEYS * PEER_NKEYS
PEER_QDIM = 256
PEER_TOPK = 16
PEER_TOK_BLOCK = 256
DN_ALPHA = (2 * DEPTH) ** 0.25
DN_BETA = (8 * DEPTH) ** -0.25
LN_EPS = 1e-5
RMS_EPS = 1e-6

GDN_QK = GDN_HEADS * GDN_DK
GDN_V = GDN_HEADS * GDN_DV
CONV_DIM = 2 * GDN_QK + GDN_V
ATT_Q = ATT_HEADS * ATT_DH
ATT_KV = ATT_KV_HEADS * ATT_DH
IDX_Q = IDX_HEADS * IDX_DIM
IN_SIZES = (CONV_DIM, GDN_V, GDN_HEADS, GDN_HEADS, ATT_Q, ATT_KV, ATT_KV, IDX_Q, IDX_DIM, IDX_HEADS, D_MODEL, D_MODEL)
IN_DIM = sum(IN_SIZES)

kernel_name = 'hybrid_gdn_dsa_peer_step'


def layer_norm(x, g, b):
    xf = x.astype(jnp.float32)
    mu = xf.mean(-1, keepdims=True)
    var = jnp.square(xf - mu).mean(-1, keepdims=True)
    return ((xf - mu) * lax.rsqrt(var + LN_EPS) * g + b).astype(x.dtype)


def l2norm(x):
    xf = x.astype(jnp.float32)
    return xf * lax.rsqrt(jnp.sum(xf * xf, -1, keepdims=True) + RMS_EPS)


def in_projection(x, w_in):
    offsets = np.cumsum(IN_SIZES)[:-1].tolist()
    return jnp.split(x @ w_in, offsets, axis=-1)


def short_conv(u, prev, conv_w):
    T = u.shape[1]
    up = jnp.concatenate([prev.astype(u.dtype), u], axis=1)
    out = up[:, 0:T] * conv_w[0]
    for i in range(1, CONV_W):
        out = out + up[:, i:i + T] * conv_w[i]
    return jax.nn.silu(out), up[:, -(CONV_W - 1):]


def gdn_features(qkv_pre, conv_prev, b_raw, a_raw, conv_w, a_log, dt_bias):
    qkv, conv_new = short_conv(qkv_pre, conv_prev, conv_w)
    B, T, _ = qkv.shape
    q, k, v = jnp.split(qkv, [GDN_QK, 2 * GDN_QK], axis=-1)
    q = l2norm(q.reshape(B, T, GDN_HEADS, GDN_DK)) * (GDN_DK ** -0.5)
    k = l2norm(k.reshape(B, T, GDN_HEADS, GDN_DK))
    v = v.reshape(B, T, GDN_HEADS, GDN_DV)
    beta = jax.nn.sigmoid(b_raw.astype(jnp.float32))
    log_a = -jnp.exp(a_log.astype(jnp.float32)) * jax.nn.softplus(a_raw.astype(jnp.float32) + dt_bias.astype(jnp.float32))
    return q, k, v, beta, log_a, conv_new


def gdn_chunk(S, q, k, v, beta, log_a):
    f32 = jnp.float32
    q, k, v = (t.astype(f32).transpose(0, 2, 1, 3) for t in (q, k, v))
    beta = beta.astype(f32).transpose(0, 2, 1)
    g = jnp.cumsum(log_a.astype(f32).transpose(0, 2, 1), axis=-1)
    C = q.shape[2]
    incl = jnp.tril(jnp.ones((C, C), dtype=bool))
    strict = jnp.tril(jnp.ones((C, C), dtype=bool), -1)
    dec_incl = jnp.exp(jnp.where(incl, g[..., :, None] - g[..., None, :], -jnp.inf))
    dec_strict = jnp.where(strict, dec_incl, 0.0)
    lower = beta[..., None] * dec_strict * jnp.einsum('bhtd,bhjd->bhtj', k, k)
    eg = jnp.exp(g)[..., None]
    rhs = beta[..., None] * (v - eg * jnp.einsum('bhtd,bhde->bhte', k, S))
    u = lax.linalg.triangular_solve(lower, rhs, left_side=True, lower=True, unit_diagonal=True)
    qk = jnp.einsum('bhtd,bhjd->bhtj', q, k)
    o = eg * jnp.einsum('bhtd,bhde->bhte', q, S) + jnp.einsum('bhtj,bhje->bhte', dec_incl * qk, u)
    g_last = g[..., -1:]
    S_new = jnp.exp(g_last)[..., None] * S + jnp.einsum('bhjd,bhje->bhde', k * jnp.exp(g_last - g)[..., None], u)
    return S_new, o.transpose(0, 2, 1, 3)


def gdn_prompt(q, k, v, beta, log_a):
    B = q.shape[0]
    S0 = jnp.zeros((B, GDN_HEADS, GDN_DK, GDN_DV), jnp.float32)
    S, o_meta = gdn_chunk(S0, q[:, :N_META], k[:, :N_META], v[:, :N_META], beta[:, :N_META], log_a[:, :N_META])

    def chunks(t):
        t = t[:, N_META:]
        return t.reshape(B, -1, GDN_CHUNK, *t.shape[2:]).swapaxes(0, 1)

    S, o_rest = lax.scan(lambda s, c: gdn_chunk(s, *c), S, tuple(chunks(t) for t in (q, k, v, beta, log_a)))
    o_rest = o_rest.swapaxes(0, 1).reshape(B, -1, GDN_HEADS, GDN_DV)
    return jnp.concatenate([o_meta, o_rest], axis=1), S


def gdn_output(o, z, norm_g):
    B, T = o.shape[:2]
    of = o * lax.rsqrt(jnp.mean(o * o, -1, keepdims=True) + RMS_EPS) * norm_g
    zz = z.reshape(B, T, GDN_HEADS, GDN_DV).astype(jnp.float32)
    return (of * jax.nn.silu(zz)).astype(z.dtype).reshape(B, T, GDN_V)


def indexer_scores(qi, ki, wi, qpos, kpos):
    s = jax.nn.relu(jnp.einsum('bqhd,bkd->bqhk', qi, ki).astype(jnp.float32)) * (IDX_DIM ** -0.5)
    I = jnp.einsum('bqhk,bqh->bqk', s, wi.astype(jnp.float32) * (IDX_HEADS ** -0.5))
    return jnp.where(kpos[None, :] <= qpos[:, None], I, -jnp.inf)


def attend_selected(q, k_sel, v_sel, valid):
    B, Q = q.shape[:2]
    qg = q.reshape(B, Q, ATT_KV_HEADS, ATT_HEADS // ATT_KV_HEADS, ATT_DH)
    s = jnp.einsum('bqhgd,bqkhd->bqhgk', qg, k_sel).astype(jnp.float32) * (ATT_DH ** -0.5)
    s = jnp.where(valid[:, :, None, None, :], s, -jnp.inf)
    p = jax.nn.softmax(s, axis=-1).astype(v_sel.dtype)
    o = jnp.einsum('bqhgk,bqkhd->bqhgd', p, v_sel)
    return o.reshape(B, Q, ATT_Q)


def gather_rows(arr, idx):
    return jax.vmap(lambda a, i: a[i])(arr, idx)


def sparse_attn_prompt(q, k, v, qi, ki, wi):
    B, L = q.shape[:2]
    n_sel = min(TOPK_MAX, L // 4)
    n_blk = -(-L // Q_BLOCK)
    pad = n_blk * Q_BLOCK - L

    def blocks(t):
        t = jnp.pad(t, [(0, 0), (0, pad)] + [(0, 0)] * (t.ndim - 2))
        return t.reshape(B, n_blk, Q_BLOCK, *t.shape[2:]).swapaxes(0, 1)

    kpos = jnp.arange(L)
    qpos_all = jnp.arange(n_blk * Q_BLOCK).reshape(n_blk, Q_BLOCK)

    def one_block(args):
        qb, qib, wib, qpos = args
        I = indexer_scores(qib, ki, wib, qpos, kpos)
        _, idx = lax.top_k(I, n_sel)
        valid = idx <= qpos[:, None]
        flat = idx.reshape(B, -1)
        k_sel = gather_rows(k, flat).reshape(B, Q_BLOCK, n_sel, ATT_KV_HEADS, ATT_DH)
        v_sel = gather_rows(v, flat).reshape(B, Q_BLOCK, n_sel, ATT_KV_HEADS, ATT_DH)
        return attend_selected(qb, k_sel, v_sel, valid)

    o = lax.map(one_block, (blocks(q), blocks(qi), blocks(wi), qpos_all))
    return o.swapaxes(0, 1).reshape(B, n_blk * Q_BLOCK, ATT_Q)[:, :L]


def sparse_attn_sample(q, k_new, v_new, qi, ki_new, wi, cache_k, cache_v, cache_ik, page_table):
    DB, T = q.shape[:2]
    past = page_table.shape[1] * PAGE_SIZE
    L = past + T
    n_sel = min(TOPK_MAX, L // 4)
    ki_past = cache_ik[page_table].reshape(DB, past, IDX_DIM).astype(ki_new.dtype)
    ki_all = jnp.concatenate([ki_past, ki_new], axis=1)
    qpos = past + jnp.arange(T)
    I = indexer_scores(qi, ki_all, wi, qpos, jnp.arange(L))
    _, idx = lax.top_k(I, n_sel)
    valid = idx <= qpos[:, None]
    flat = idx.reshape(DB, -1)
    in_past = flat < past
    page = gather_rows(page_table, jnp.minimum(flat, past - 1) // PAGE_SIZE)
    phys = page * PAGE_SIZE + flat % PAGE_SIZE
    new_pos = jnp.clip(flat - past, 0, T - 1)

    def gather(pool, new):
        from_pool = pool.reshape(-1, ATT_KV_HEADS, ATT_DH)[phys].astype(new.dtype)
        from_new = gather_rows(new, new_pos)
        return jnp.where(in_past[:, :, None, None], from_pool, from_new).reshape(DB, T, n_sel, ATT_KV_HEADS, ATT_DH)

    return attend_selected(q, gather(cache_k, k_new), gather(cache_v, v_new), valid)


def peer_ffn(h, peer_wq, peer_subkeys, peer_u, peer_v):
    N = h.shape[0]
    n_blk = -(-N // PEER_TOK_BLOCK)
    hb = jnp.pad(h, ((0, n_blk * PEER_TOK_BLOCK - N), (0, 0))).reshape(n_blk, PEER_TOK_BLOCK, D_MODEL)

    def one_block(xb):
        T = xb.shape[0]
        qry = (xb @ peer_wq).reshape(T, PEER_HEADS, 2, PEER_QDIM // 2)
        s = jnp.einsum('thcd,hcnd->thcn', qry, peer_subkeys).astype(jnp.float32)
        sv, si = lax.top_k(s, PEER_TOPK)
        cand_s = (sv[:, :, 0, :, None] + sv[:, :, 1, None, :]).reshape(T, PEER_HEADS, PEER_TOPK * PEER_TOPK)
        cand_i = (si[:, :, 0, :, None] * PEER_NKEYS + si[:, :, 1, None, :]).reshape(T, PEER_HEADS, PEER_TOPK * PEER_TOPK)
        top_s, pos = lax.top_k(cand_s, PEER_TOPK)
        expert = jnp.take_along_axis(cand_i, pos, axis=-1)
        gate = jax.nn.softmax(top_s, axis=-1)
        act = jax.nn.gelu(jnp.einsum('thkd,td->thk', peer_u[expert], xb).astype(jnp.float32), approximate=False)
        return jnp.einsum('thk,thkd->td', (gate * act).astype(xb.dtype), peer_v[expert])

    return lax.map(one_block, hb).reshape(-1, D_MODEL)[:N]


def finish_layer(x, o_gdn, o_att, ga, gb, w_bg, w_ba, w_out, ln1_g, ln1_b, peer_wq, peer_subkeys, peer_u, peer_v, ln2_g, ln2_b):
    merged = jax.nn.sigmoid(ga) * (o_gdn @ w_bg) + jax.nn.sigmoid(gb) * (o_att @ w_ba)
    h = layer_norm(DN_ALPHA * x + merged @ w_out, ln1_g, ln1_b)
    f = peer_ffn(h.reshape(-1, D_MODEL), peer_wq, peer_subkeys, peer_u, peer_v).reshape(h.shape)
    return layer_norm(DN_ALPHA * h + f, ln2_g, ln2_b)


def setup_inputs(seed: int = 0) -> dict:
    key = jax.random.key(seed)
    ks = iter(jax.random.split(key, 40))
    f32 = jnp.float32

    def nrm(shape, scale):
        return jax.random.normal(next(ks), shape, f32) * scale

    n_pages = PAST_LEN // PAGE_SIZE
    n_used = DEC_BATCH * n_pages
    n_pool = n_used + n_used // 4
    x_prompt = nrm((BATCH, SEQ, D_MODEL), 1.0)
    x_sample = nrm((DEC_BATCH, DEC_SEQ, D_MODEL), 1.0)
    cache_k = nrm((DEPTH, n_pool, PAGE_SIZE, ATT_KV_HEADS, ATT_DH), 1.0)
    cache_v = nrm((DEPTH, n_pool, PAGE_SIZE, ATT_KV_HEADS, ATT_DH), 1.0)
    cache_idx_k = nrm((DEPTH, n_pool, PAGE_SIZE, IDX_DIM), 1.0)
    state_conv = nrm((DEPTH, DEC_BATCH, CONV_W - 1, CONV_DIM), 1.0)
    state_delta = nrm((DEPTH, DEC_BATCH, GDN_HEADS, GDN_DK, GDN_DV), 0.05)
    page_table = jax.random.permutation(next(ks), n_pool)[:n_used].reshape(DEC_BATCH, n_pages).astype(jnp.int32)
    meta_tokens = nrm((N_META, D_MODEL), 1.0)
    w_in = nrm((DEPTH, D_MODEL, IN_DIM), D_MODEL ** -0.5)
    conv_w = nrm((DEPTH, CONV_W, CONV_DIM), CONV_W ** -0.5)
    a_log = jnp.log(jax.random.uniform(next(ks), (DEPTH, GDN_HEADS), f32, 1.0, 16.0))
    dt = jnp.exp(jax.random.uniform(next(ks), (DEPTH, GDN_HEADS), f32, math.log(1e-3), math.log(1e-1)))
    dt_bias = dt + jnp.log(-jnp.expm1(-dt))
    gdn_norm_g = 1.0 + nrm((DEPTH, GDN_DV), 0.02)
    w_branch_gdn = nrm((DEPTH, GDN_V, D_MODEL), GDN_V ** -0.5)
    w_branch_attn = nrm((DEPTH, ATT_Q, D_MODEL), ATT_Q ** -0.5)
    w_out = nrm((DEPTH, D_MODEL, D_MODEL), DN_BETA * D_MODEL ** -0.5)
    ln1_g = 1.0 + nrm((DEPTH, D_MODEL), 0.02)
    ln1_b = nrm((DEPTH, D_MODEL), 0.02)
    peer_wq = nrm((DEPTH, D_MODEL, PEER_HEADS * PEER_QDIM), D_MODEL ** -0.5)
    peer_subkeys = nrm((DEPTH, PEER_HEADS, 2, PEER_NKEYS, PEER_QDIM // 2), (PEER_QDIM // 2) ** -0.5)
    peer_u = nrm((DEPTH, PEER_EXPERTS, D_MODEL), D_MODEL ** -0.5)
    peer_v = nrm((DEPTH, PEER_EXPERTS, D_MODEL), DN_BETA * PEER_HEADS ** -0.5)
    ln2_g = 1.0 + nrm((DEPTH, D_MODEL), 0.02)
    ln2_b = nrm((DEPTH, D_MODEL), 0.02)
    return {'x_prompt': x_prompt, 'x_sample': x_sample, 'cache_k': cache_k, 'cache_v': cache_v,
            'cache_idx_k': cache_idx_k, 'state_conv': state_conv, 'state_delta': state_delta,
            'page_table': page_table, 'meta_tokens': meta_tokens, 'w_in': w_in, 'conv_w': conv_w,
            'a_log': a_log, 'dt_bias': dt_bias, 'gdn_norm_g': gdn_norm_g, 'w_branch_gdn': w_branch_gdn,
            'w_branch_attn': w_branch_attn, 'w_out': w_out, 'ln1_g': ln1_g, 'ln1_b': ln1_b,
            'peer_wq': peer_wq, 'peer_subkeys': peer_subkeys, 'peer_u': peer_u, 'peer_v': peer_v,
            'ln2_g': ln2_g, 'ln2_b': ln2_b}


def reference(x_prompt, x_sample, cache_k, cache_v, cache_idx_k, state_conv, state_delta, page_table,
              meta_tokens, w_in, conv_w, a_log, dt_bias, gdn_norm_g, w_branch_gdn, w_branch_attn, w_out,
              ln1_g, ln1_b, peer_wq, peer_subkeys, peer_u, peer_v, ln2_g, ln2_b):
    B = x_prompt.shape[0]
    DB, T = x_sample.shape[:2]
    hp = jnp.concatenate([jnp.broadcast_to(meta_tokens[None].astype(x_prompt.dtype), (B, N_META, D_MODEL)), x_prompt], axis=1)
    hs = x_sample
    Lp = hp.shape[1]
    p_conv_l, p_delta_l, p_k_l, p_v_l, p_ik_l = [], [], [], [], []
    s_conv_l, s_delta_l, s_k_l, s_v_l, s_ik_l = [], [], [], [], []
    for l in range(DEPTH):
        qkv_pre, z, b_raw, a_raw, aq, ak, av, iq, ik, iw, ga, gb = in_projection(hp, w_in[l])
        zero_prev = jnp.zeros((B, CONV_W - 1, CONV_DIM), hp.dtype)
        q, k, v, beta, log_a, p_conv = gdn_features(qkv_pre, zero_prev, b_raw, a_raw, conv_w[l], a_log[l], dt_bias[l])
        o_gdn, p_S = gdn_prompt(q, k, v, beta, log_a)
        o_gdn = gdn_output(o_gdn, z, gdn_norm_g[l])
        pk = ak.reshape(B, Lp, ATT_KV_HEADS, ATT_DH)
        pv = av.reshape(B, Lp, ATT_KV_HEADS, ATT_DH)
        o_att = sparse_attn_prompt(aq.reshape(B, Lp, ATT_HEADS, ATT_DH), pk, pv,
                                   iq.reshape(B, Lp, IDX_HEADS, IDX_DIM), ik, iw)
        hp = finish_layer(hp, o_gdn, o_att, ga, gb, w_branch_gdn[l], w_branch_attn[l], w_out[l], ln1_g[l], ln1_b[l],
                          peer_wq[l], peer_subkeys[l], peer_u[l], peer_v[l], ln2_g[l], ln2_b[l])
        p_conv_l.append(p_conv)
        p_delta_l.append(p_S.astype(state_delta.dtype))
        p_k_l.append(pk)
        p_v_l.append(pv)
        p_ik_l.append(ik)
        qkv_pre, z, b_raw, a_raw, aq, ak, av, iq, ik, iw, ga, gb = in_projection(hs, w_in[l])
        q, k, v, beta, log_a, s_conv = gdn_features(qkv_pre, state_conv[l], b_raw, a_raw, conv_w[l], a_log[l], dt_bias[l])
        s_S, o_gdn = gdn_chunk(state_delta[l].astype(jnp.float32), q, k, v, beta, log_a)
        o_gdn = gdn_output(o_gdn, z, gdn_norm_g[l])
        sk = ak.reshape(DB, T, ATT_KV_HEADS, ATT_DH)
        sv = av.reshape(DB, T, ATT_KV_HEADS, ATT_DH)
        o_att = sparse_attn_sample(aq.reshape(DB, T, ATT_HEADS, ATT_DH), sk, sv,
                                   iq.reshape(DB, T, IDX_HEADS, IDX_DIM), ik, iw,
                                   cache_k[l], cache_v[l], cache_idx_k[l], page_table)
        hs = finish_layer(hs, o_gdn, o_att, ga, gb, w_branch_gdn[l], w_branch_attn[l], w_out[l], ln1_g[l], ln1_b[l],
                          peer_wq[l], peer_subkeys[l], peer_u[l], peer_v[l], ln2_g[l], ln2_b[l])
        s_conv_l.append(s_conv)
        s_delta_l.append(s_S.astype(state_delta.dtype))
        s_k_l.append(sk)
        s_v_l.append(sv)
        s_ik_l.append(ik)
    y_prompt = hp[:, N_META:]
    y_sample = hs
    p_conv = jnp.stack(p_conv_l)
    p_delta = jnp.stack(p_delta_l)
    p_k = jnp.stack(p_k_l)
    p_v = jnp.stack(p_v_l)
    p_idx_k = jnp.stack(p_ik_l)
    s_conv = jnp.stack(s_conv_l)
    s_delta = jnp.stack(s_delta_l)
    s_k = jnp.stack(s_k_l)
    s_v = jnp.stack(s_v_l)
    s_idx_k = jnp.stack(s_ik_l)
    return (y_prompt, y_sample, p_conv, p_delta, p_k, p_v, p_idx_k, s_conv, s_delta, s_k, s_v, s_idx_k)
```

```python
import contextlib
import numpy as np
import concourse.bass as bass
import concourse.mybir as mybir
from concourse.bass_utils import run_bass_kernel_spmd

F32 = mybir.dt.float32
BF16 = mybir.dt.bfloat16
I32 = mybir.dt.int32
U32 = mybir.dt.uint32
AF = mybir.ActivationFunctionType
ALU = mybir.AluOpType
AX = mybir.AxisListType

NCORES = 8
D = 1024
SEQ = 2048
NMETA = 16
LP = SEQ + NMETA
NS = 128
NT = LP + NS
IN_DIM = 8800
C_QKV, C_Z, C_B, C_A, C_AQ, C_AK, C_AV, C_IQ, C_IK, C_IW, C_GA, C_GB = (
    0, 3072, 4096, 4104, 4112, 5136, 5392, 5648, 6672, 6736, 6752, 7776)


class _Op:
    __slots__ = ("eng", "fn", "waits", "sig", "is_dma", "dma_slot", "dma_cnt", "idx", "drain")


class Prog:
    ENGS = ("sync", "gpsimd", "scalar", "vector", "tensor")
    RING = 16

    def __init__(self, nc):
        self.nc = nc
        self.ops = []
        self.state = {}

    def _conflicts(self, key):
        name, idx = key
        d = self.state.setdefault(name, {})
        if idx is None:
            return list(d.values())
        out = []
        if None in d:
            out.append(d[None])
        if idx in d:
            out.append(d[idx])
        return out

    def _entry(self, key):
        name, idx = key
        d = self.state.setdefault(name, {})
        if key[1] not in d:
            d[idx] = [None, {}]
        return d[idx]

    @staticmethod
    def _norm(keys):
        out = []
        for k in keys or ():
            if isinstance(k, str):
                out.append((k, None))
            else:
                out.append((k[0], k[1]))
        return out

    def op(self, eng, fn, reads=(), writes=(), is_dma=False):
        o = _Op()
        o.eng, o.fn, o.is_dma = eng, fn, is_dma
        o.sig = False
        o.drain = False
        o.idx = len(self.ops)
        deps = set()
        reads, writes = self._norm(reads), self._norm(writes)
        writes = writes + [k for k in reads if k[0].startswith("pb")]
        reads = [k for k in reads if not k[0].startswith("pb")]
        for k in reads:
            for st in self._conflicts(k):
                if st[0] is not None:
                    deps.add(st[0])
        for k in writes:
            for st in self._conflicts(k):
                if st[0] is not None:
                    deps.add(st[0])
                for r in st[1].values():
                    deps.add(r)
        deps.discard(o.idx)
        o.waits = deps
        for k in reads:
            self._entry(k)[1][(eng, o.idx) if is_dma else (eng, -1)] = o.idx
        for k in writes:
            name, idx = k
            if idx is None:
                self.state[name] = {None: [o.idx, {}]}
            else:
                e = self._entry(k)
                e[0] = o.idx
                e[1] = {}
        self.ops.append(o)
        return o

    def barrier(self, tiny):
        for e in self.ENGS:
            o = self.op(e, tiny[e], reads=(), writes=[("bar_" + e, None)] + ([("pb6", None)] if e == "tensor" else []))
            o.drain = True
        for e in self.ENGS:
            self.op(e, tiny[e], reads=[("bar_" + x, None) for x in self.ENGS if x != e],
                    writes=[("bar2_" + e, None)] + ([("pb6", None)] if e == "tensor" else []))

    def dma(self, eng, out, in_, reads=(), writes=(), **kw):
        return self.op(eng, lambda e: e.dma_start(out=out, in_=in_, **kw), reads, writes, is_dma=True)

    def emit(self, es):
        nc = self.nc
        ops = self.ops
        for o in ops:
            keep = set()
            for d in o.waits:
                p = ops[d]
                if p.eng == o.eng and not p.is_dma:
                    if o.eng == "tensor":
                        continue
                    if o.eng in ("sync",):
                        continue
                keep.add(d)
            o.waits = keep
            for d in keep:
                ops[d].sig = True
        eng_sem = {e: es.enter_context(nc.semaphore("s_" + e)) for e in self.ENGS}
        rings = {e: [es.enter_context(nc.semaphore("r_%s_%d" % (e, i))) for i in range(self.RING)]
                 for e in ("sync", "gpsimd")}
        cnt = {e: 0 for e in self.ENGS}
        ring_n = {e: 0 for e in rings}
        for o in ops:
            if o.is_dma:
                n = ring_n[o.eng]
                ring_n[o.eng] += 1
                o.dma_slot = n % self.RING
                o.dma_cnt = 16 * (n // self.RING + 1)
            elif o.sig:
                cnt[o.eng] += 1
                o.dma_cnt = cnt[o.eng]
        per_eng = {e: [o for o in ops if o.eng == e] for e in self.ENGS}

        def run(e, engobj):
            seen = {}
            latest = {}
            for o in per_eng[e]:
                need = {}
                if o.drain:
                    for sl, v in latest.items():
                        need[("d", e, sl)] = v
                for d in o.waits:
                    p = ops[d]
                    if p.is_dma:
                        key = ("d", p.eng, p.dma_slot)
                    else:
                        key = ("e", p.eng)
                    need[key] = max(need.get(key, 0), p.dma_cnt)
                if o.is_dma:
                    if o.dma_cnt > 16:
                        key = ("d", o.eng, o.dma_slot)
                        need[key] = max(need.get(key, 0), o.dma_cnt - 16)
                for key, v in need.items():
                    if seen.get(key, 0) >= v:
                        continue
                    seen[key] = v
                    sem = rings[key[1]][key[2]] if key[0] == "d" else eng_sem[key[1]]
                    engobj.wait_ge(sem, v)
                ins = o.fn(engobj)
                if o.is_dma:
                    latest[o.dma_slot] = o.dma_cnt
                    ins.then_inc(rings[o.eng][o.dma_slot], 16)
                elif o.sig:
                    ins.then_inc(eng_sem[o.eng], 1)
            if e in rings:
                for s in range(self.RING):
                    n = ring_n[e]
                    uses = n // self.RING + (1 if s < n % self.RING else 0)
                    if uses > 0 and seen.get(("d", e, s), 0) < 16 * uses:
                        engobj.wait_ge(rings[e][s], 16 * uses)

        block = es.enter_context(nc.Block())

        @block.sync
        def _(eng):
            run("sync", eng)

        @block.gpsimd
        def _(eng):
            run("gpsimd", eng)

        @block.scalar
        def _(eng):
            run("scalar", eng)

        @block.vector
        def _(eng):
            run("vector", eng)

        @block.tensor
        def _(eng):
            run("tensor", eng)


NTC = 2243
OFFP = 3
OFFS = 2115
NCONST = 2081
K_SH = 1168
K_IOTA = 1552
K_MT = 1680
K_SEL8 = 1808
K_MNEW = 1936
K_PCOL = 1944
K_BD = 1952
K_IOTA16 = 1960
(K_ID, K_TRIP, K_ONESP, K_MLP, K_MLIP, K_TRIS, K_ONESS, K_MLS, K_MLIS, K_BM) = (
    0, 128, 256, 384, 512, 640, 768, 896, 1024, 1152)
RMS_EPS = 1e-6


def make_consts():
    c = np.zeros((128, NCONST), np.float32)
    idx = np.arange(128)
    c[:, K_ID:K_ID + 128] = np.eye(128)
    blk = idx // 64
    same = blk[:, None] == blk[None, :]
    c[:, K_TRIP:K_TRIP + 128] = same & (idx[:, None] <= idx[None, :])
    c[:, K_ONESP:K_ONESP + 128] = same
    c[:, K_MLP:K_MLP + 128] = same & (idx[None, :] < idx[:, None])
    c[:, K_MLIP:K_MLIP + 128] = same & (idx[None, :] <= idx[:, None])
    s_ = idx % 16
    t_ = idx // 16
    sames = s_[:, None] == s_[None, :]
    c[:, K_TRIS:K_TRIS + 128] = sames & (t_[:, None] <= t_[None, :])
    c[:, K_ONESS:K_ONESS + 128] = sames
    c[:, K_MLS:K_MLS + 128] = sames & (t_[None, :] < t_[:, None])
    c[:, K_MLIS:K_MLIS + 128] = sames & (t_[None, :] <= t_[:, None])
    c[:, K_BM:K_BM + 16] = s_[:, None] == np.arange(16)[None, :]
    c[:, K_IOTA:K_IOTA + 128] = idx[None, :]
    c[:, K_MT:K_MT + 128] = (idx % 8)[:, None] == (idx // 16)[None, :]
    c[0:8, K_SEL8:K_SEL8 + 128] = np.arange(8)[:, None] == (idx // 16)[None, :]
    c[:, K_MNEW:K_MNEW + 8] = np.arange(8)[None, :] <= (idx // 16)[:, None]
    c[:, K_PCOL] = idx
    c[:, K_BD:K_BD + 8] = (idx // 16)[:, None] == np.arange(8)[None, :]
    c[:, K_IOTA16:K_IOTA16 + 16] = np.arange(16)[None, :]
    for i in range(3):
        for r in range(3):
            for sq in range(16):
                if r - i >= 0:
                    c[16 * r + sq, K_SH + 128 * i + 16 * (r - i) + sq] = 1.0
    return c


def build():
    import os
    KSTOP = int(os.environ.get('KSTOP', '99'))
    KTILES = int(os.environ.get('KTILES', '99'))
    KC = int(os.environ.get('KC', '7'))
    nc = bass.Bass("TRN2", target_bir_lowering=False)
    P = Prog(nc)
    es = contextlib.ExitStack()

    def din(name, shape, dt=F32):
        return nc.dram_tensor(name, list(shape), dt, kind="ExternalInput").ap()

    def dout(name, shape, dt=F32):
        return nc.dram_tensor(name, list(shape), dt, kind="ExternalOutput").ap()

    es_ph = [contextlib.ExitStack()]

    def sb(name, shape, dt=F32):
        return es.enter_context(nc.sbuf_tensor(name, list(shape), dt))

    stk = [contextlib.ExitStack() for _ in range(4)]
    uid = [0]

    def alloc(level, name, shape, dt=F32):
        uid[0] += 1
        if os.environ.get("KMEM"):
            print("alloc", level, name, shape, "free", nc.sbuf_bytes_remaining)
        return stk[level].enter_context(nc.sbuf_tensor("%s_u%d" % (name, uid[0]), list(shape), dt))

    def sbq(name, shape, dt=F32):
        return alloc(3, name, shape, dt)

    def sbp(name, shape, dt=F32):
        return es_ph[0].enter_context(nc.sbuf_tensor(name, list(shape), dt))

    xT = din("xT", [D, NTC])
    w_in = din("w_in", [D, IN_DIM])
    conv_w = din("conv_w", [4, 3072])
    consts_d = din("consts", [128, NCONST])
    params_d = din("params", [128, 144])
    o_pk = dout("p_k", [LP, 256])
    o_pv = dout("p_v", [LP, 256])
    o_pik = dout("p_idx_k", [LP, 64])
    o_sk = dout("s_k", [NS, 256])
    o_sv = dout("s_v", [NS, 256])
    o_sik = dout("s_idx_k", [NS, 64])
    o_pconv = dout("p_conv", [3, 3072])
    o_sconv = dout("s_conv", [NS, 3072])
    o_pdelta = dout("p_delta", [8, 128, 128])
    sconv_d = din("state_conv", [16, 3, 3072])
    sdelta_d = din("state_delta", [16, 8, 128, 128])
    o_sdelta = dout("s_delta", [16, 8, 128, 128])
    ptab_d = din("page_table", [16, 16], I32)
    ck_d = din("cache_k", [2560 * 128, 256])
    cv_d = din("cache_v", [2560 * 128, 256])
    cik_d = din("cache_ik", [2560 * 128, 64])
    wbg_d = din("w_bg", [D, D])
    wba_d = din("w_ba", [D, D])
    wout_d = din("w_out", [D, D])
    lnp_d = din("lnp", [128, 4096])
    xstm_d = din("xs_tm", [128, D])
    wq_d = din("peer_wq", [D, 2048])
    skT_d = din("skT", [128, 16, 128])
    uT_d = din("peer_uT", [128, D, 128])
    pv_d = din("peer_v", [16384, D])
    o_ys = dout("y_sample", [128, D])
    NPB = int(os.environ.get('KNPB', '17'))
    xptm_d = din("xp_tm", [2176, D])
    o_yp = dout("y_prompt", [2176, D])
    KDBG = int(os.environ.get('KDBG', '0'))
    o_dbg = dout("dbg", [40, 128, 512]) if KDBG else None
    dbg_n = [0]
    dbg_names = []

    def dbg(name, ap, key):
        if not KDBG or dbg_n[0] >= 40:
            return
        i = dbg_n[0]
        dbg_n[0] += 1
        dbg_names.append(name)
        shp = ap.shape
        dst = o_dbg[i, 0:shp[0], 0:int(np.prod(shp[1:]))]
        if len(shp) == 3:
            dst = dst.rearrange("p (a b) -> p a b", a=shp[1])
        P.dma("sync", dst, ap, reads=[key])

    xTb = sb("xTb", [128, 8, NTC], BF16)
    ogT = sb("ogT", [128, 8, NTC], BF16)
    cst = sb("cst", [128, NCONST])
    prm = sb("prm", [128, 144])
    onesA = sb("onesA", [128, 128])
    dummy = sb("dmy0", [128, 8])
    stage = [sbp("stage%d" % i, [128, 8, 256]) for i in range(2)]
    obuf = [sbp("obuf%d" % i, [128, 512]) for i in range(2)]
    Wg = sbp("Wg", [128, 8, 2048], BF16)
    pbs = [es.enter_context(nc.psum_tensor("pb%d" % i, [128, 512], F32)) for i in range(8)]

    def V(fn, r=(), w=()):
        P.op("vector", fn, r, w)

    def A(fn, r=(), w=()):
        P.op("scalar", fn, r, w)

    def G(fn, r=(), w=()):
        P.op("gpsimd", fn, r, w)

    def MM(out, lhsT, rhs, start=True, stop=True, r=(), w=()):
        P.op("tensor", lambda e: e.matmul(out, lhsT=lhsT, rhs=rhs, start=start, stop=stop), r, w)

    def TR(out, in_, ident, r=(), w=()):
        P.op("tensor", lambda e: e.matmul(out, lhsT=in_, rhs=ident, start=True, stop=True), r, w)

    def tt(engf, out, in0, in1, op, r=(), w=()):
        engf(lambda e: e.tensor_tensor(out=out, in0=in0, in1=in1, op=op), r, w)

    def ts(engf, out, in0, s1, op0, s2=None, op1=None, r=(), w=()):
        if op1 is None:
            engf(lambda e: e.tensor_scalar(out=out, in0=in0, scalar1=s1, scalar2=None, op0=op0), r, w)
        else:
            engf(lambda e: e.tensor_scalar(out=out, in0=in0, scalar1=s1, scalar2=s2, op0=op0, op1=op1), r, w)

    def act(out, in_, func, r=(), w=(), **kw):
        A(lambda e: e.activation(out=out, in_=in_, func=func, **kw), r, w)

    def vcopy(out, in_, r=(), w=()):
        V(lambda e: e.tensor_copy(out=out, in_=in_), r, w)

    def acopy(out, in_, r=(), w=()):
        A(lambda e: e.copy(out=out, in_=in_), r, w)

    def gcopy(out, in_, r=(), w=()):
        G(lambda e: e.tensor_copy(out=out, in_=in_), r, w)

    st = {"dq": 0, "bank": 0, "w": 0, "ob": 0, "stg": 0}

    def dmaq():
        st["dq"] += 1
        return "sync" if st["dq"] % 2 == 0 else "gpsimd"

    def bank():
        i = st["bank"] % 6
        st["bank"] += 1
        return pbs[i], ("pb%d" % i, None)

    P.dma("sync", cst[:, :], consts_d[:, :], writes=["cst"])
    P.dma("sync", prm[:, :], params_d[:, :], writes=["prm"])
    V(lambda e: e.memset(onesA[:, :], 1.0), w=["onesA"])
    V(lambda e: e.memset(ogT[:, :, 0:OFFP], 0.0), w=[("ogT", "gap0")])
    V(lambda e: e.memset(ogT[:, :, OFFP + LP:OFFS], 0.0), w=[("ogT", "gap1")])
    ident = cst[:, K_ID:K_ID + 128]

    xT_v = xT.rearrange("(k p) t -> p k t", p=128)
    t0 = 0
    ci = 0
    while t0 < NTC:
        n = min(256, NTC - t0)
        i = st["stg"]
        st["stg"] += 1
        sg = stage[i % 2]
        P.dma(dmaq(), sg[:, :, 0:n], xT_v[:, :, t0:t0 + n], writes=[("stage", i % 2)])
        for k in range(8):
            if k % 2 == 0:
                vcopy(xTb[:, k, t0:t0 + n], sg[:, k, 0:n], r=[("stage", i % 2)], w=[("xTb", (k, ci))])
            else:
                acopy(xTb[:, k, t0:t0 + n], sg[:, k, 0:n], r=[("stage", i % 2)], w=[("xTb", (k, ci))])
        t0 += n
        ci += 1

    w_v = w_in.rearrange("(k p) c -> p k c", p=128)

    def load_cast(dst, dkey, c0, ncol, d0=0):
        for o_ in range(0, ncol, 256):
            n_ = min(256, ncol - o_)
            i = st["stg"]
            st["stg"] += 1
            sg = stage[i % 2]
            P.dma(dmaq(), sg[:, :, 0:n_], w_v[:, :, c0 + o_:c0 + o_ + n_], writes=[("stage", i % 2)])
            for k in range(8):
                if k % 2 == 0:
                    vcopy(dst[:, k, d0 + o_:d0 + o_ + n_], sg[:, k, 0:n_], r=[("stage", i % 2)], w=[(dkey, (k, d0 + o_))])
                else:
                    acopy(dst[:, k, d0 + o_:d0 + o_ + n_], sg[:, k, 0:n_], r=[("stage", i % 2)], w=[(dkey, (k, d0 + o_))])

    def tok_major_mm(w, wkey, ncol, col0, nt, sinks):
        pt, pkey = bank()
        for k in range(8):
            MM(pt[0:nt, 0:ncol], xTb[:, k, col0:col0 + nt], w[:, k, 0:ncol], start=(k == 0), stop=(k == 7),
               r=["xTb", wkey], w=[pkey])
        oi = st["ob"] % 2
        st["ob"] += 1
        ob = obuf[oi]
        if oi % 2 == 0:
            vcopy(ob[0:nt, 0:ncol], pt[0:nt, 0:ncol], r=[pkey], w=[("obuf", oi)])
        else:
            acopy(ob[0:nt, 0:ncol], pt[0:nt, 0:ncol], r=[pkey], w=[("obuf", oi)])
        for (dst, c0, wd) in sinks:
            P.dma(dmaq(), dst, ob[0:nt, c0:c0 + wd], reads=[("obuf", oi)])

    ttiles = []
    t0 = 0
    while t0 < LP:
        n = min(128, LP - t0)
        ttiles.append((t0, n))
        t0 += n

    def stage1():
        w, wkey = Wg[:, :, 0:512], "Wg"
        load_cast(Wg, "Wg", C_AK, 512, 0)
        for (t0, n) in ttiles:
            tok_major_mm(w, wkey, 512, OFFP + t0, n, [(o_pk[t0:t0 + n, :], 0, 256), (o_pv[t0:t0 + n, :], 256, 256)])
        tok_major_mm(w, wkey, 512, OFFS, NS, [(o_sk[:, :], 0, 256), (o_sv[:, :], 256, 256)])
        w, wkey = Wg[:, :, 512:1024], "Wg"
        load_cast(Wg, "Wg", C_IK, 64, 512)
        for (t0, n) in ttiles:
            tok_major_mm(w, wkey, 64, OFFP + t0, n, [(o_pik[t0:t0 + n, :], 0, 64)])
        tok_major_mm(w, wkey, 64, OFFS, NS, [(o_sik[:, :], 0, 64)])
        for cg in range(6):
            w, wkey = Wg[:, :, (cg % 2) * 512 + 1024:(cg % 2) * 512 + 1536], "Wg"
            load_cast(Wg, "Wg", C_QKV + cg * 512, 512, (cg % 2) * 512 + 1024)
            tok_major_mm(w, wkey, 512, OFFP + LP - 3, 3, [(o_pconv[:, cg * 512:(cg + 1) * 512], 0, 512)])
            tok_major_mm(w, wkey, 512, OFFS, NS, [(o_sconv[:, cg * 512:(cg + 1) * 512], 0, 512)])

    stage1()

    Wba = sbp("Wba", [128, 8, 16], BF16)
    cwb = [sbp("cwb%d" % i, [128, 4, 512]) for i in range(2)]
    qkv = sbp("qkv", [128, 3, 512])
    tmpc = [sbp("tmpc%d" % i, [128, 512]) for i in range(2)]
    zs = sbp("zs", [128, 512])
    sqs = sbp("sqs", [128, 512])
    sm = sbp("sm", [128, 64])
    eglb = sbp("eglb", [128, 2, 4])
    nAe = sbp("nAe", [128, 8])
    smR = tmpc[0][:, 0:64]
    egls = tmpc[0][:, 64:128]
    kT = sbp("kT", [128, 4, 128])
    qT = sbp("qT", [128, 4, 128])
    qTp = [sbp("qTp%d" % i, [128, 4, 128]) for i in range(2)]
    diag = sbp("diag", [128, 4, 128])
    Dm = sbp("Dm", [128, 4, 128])
    nbm = diag
    Am = sbp("Am", [128, 4, 128])
    AT = sbp("AT", [128, 4, 128])
    Xb = [sbp("X%d" % i, [128, 4, 128]) for i in range(2)]
    XTb = [sbp("XT%d" % i, [128, 4, 128]) for i in range(2)]
    Pmb = [sbp("Pm%d" % i, [128, 4, 128]) for i in range(2)]
    TTb = sbp("TTb", [128, 4, 128])
    TTs = sbp("TTs", [128, 4, 128])
    U0 = sbp("U0", [128, 4, 128])
    W0T = sbp("W0T", [128, 4, 128])
    ub = sbp("u", [128, 4, 128])
    ud = sbp("ud", [128, 4, 128])
    S = sbp("S", [128, 4, 128])
    AUs = Am
    ob_ = sbp("o", [128, 4, 128])

    (SM_BETA, SM_NBETA, SM_T1, SM_LA, SM_G, SM_EG, SM_DL, SM_SSQ, SM_RR, SM_SS2, SM_RSTD, SM_BEG) = (
        0, 4, 8, 12, 16, 20, 24, 28, 36, 44, 48, 52)

    def b3(t):
        return t[:, :].rearrange("p (h d) -> p h d", h=4)

    def bc_last(ap2, n):
        return ap2.unsqueeze(2).to_broadcast([ap2.shape[0], ap2.shape[1], n])

    def bc_mid(ap2, h):
        return ap2.unsqueeze(1).to_broadcast([ap2.shape[0], h, ap2.shape[1]])

    act(nAe[:, :], prm[:, 0:8], AF.Exp, r=["prm"], w=["nAe"])
    ts(V, nAe[:, :], nAe[:, :], -1.0, ALU.mult, r=["nAe"], w=["nAe"])
    for i in range(2):
        V(lambda e, i=i: e.memset(qTp[i][:, :, :], 0.0), w=["qTp%d" % i])
    load_cast(Wba, "Wba", C_B, 16)

    pb6, k6 = pbs[6], ("pb6", None)
    pb7, k7 = pbs[7], ("pb7", None)

    def gdn_tile(hg, col0, nt, nb, bs, kc, S_list, out_col, sample=False):
        k_tri, k_ones, k_ml, k_mli = kc
        hs = slice(hg * 4, hg * 4 + 4)
        idn = ident[0:nt, 0:nt]
        for gi in range(3):
            c0 = gi * 1024 + hg * 512
            ci_ = st["w"]; st["w"] += 1
            cwt, ckey = cwb[ci_ % 2], "cwb%d" % (ci_ % 2)
            P.dma(dmaq(), cwt[:, :, :], conv_w[:, c0:c0 + 512].partition_broadcast(128), writes=[ckey])
            banks = [bank() for _ in range(4)]
            if sample:
                oi_ = st["ob"] % 2
                st["ob"] += 1
                stg_, skey_ = obuf[oi_], ("obuf", oi_)
                for r_ in range(3):
                    P.dma(dmaq(), stg_[16 * r_:16 * r_ + 16, :], sconv_d[:, r_, c0:c0 + 512], writes=[skey_])
            for i in range(4):
                pt, pk = banks[i]
                extra = sample and i < 3
                for k in range(8):
                    MM(pt[0:nt, :], xTb[:, k, col0 - 3 + i:col0 - 3 + i + nt] if not sample else
                       xTb[:, k, col0 - 48 + 16 * i:col0 - 48 + 16 * i + nt],
                       Wg[:, k, gi * 512:(gi + 1) * 512], start=(k == 0), stop=(k == 7 and not extra),
                       r=["xTb", "Wg"], w=[pk])
                if extra:
                    MM(pt[0:nt, :], cst[0:48, K_SH + 128 * i:K_SH + 128 * i + 128], stg_[0:48, :], start=False, stop=True,
                       r=["cst", skey_], w=[pk])
            dst = qkv[0:nt, gi, :]
            qk_ = ("qkv", gi)
            tt(V, dst, banks[0][0][0:nt, :], cwt[0:nt, 0, :], ALU.mult, r=[banks[0][1], ckey], w=[qk_])
            for i in range(1, 4):
                tc_, tk_ = tmpc[i % 2], "tmpc%d" % (i % 2)
                tt(V, tc_[0:nt, :], banks[i][0][0:nt, :], cwt[0:nt, i, :], ALU.mult, r=[banks[i][1], ckey], w=[tk_])
                tt(G, dst, dst, tc_[0:nt, :], ALU.add, r=[qk_, tk_], w=[qk_])
            act(dst, dst, AF.Silu, r=[qk_], w=[qk_])
        if KSTOP <= 1:
            return None
        pt, pk = bank()
        for k in range(8):
            MM(pt[0:nt, :], xTb[:, k, col0:col0 + nt], Wg[:, k, 1536:2048], start=(k == 0), stop=(k == 7),
               r=["xTb", "Wg"], w=[pk])
        act(zs[0:nt, :], pt[0:nt, :], AF.Silu, r=[pk], w=["zs"])
        for k in range(8):
            MM(pb6[0:nt, 0:16], xTb[:, k, col0:col0 + nt], Wba[:, k, 0:16], start=(k == 0), stop=(k == 7),
               r=["xTb", "Wba"], w=[k6])
        beta = sm[0:nt, SM_BETA:SM_BETA + 4]
        nbeta = sm[0:nt, SM_NBETA:SM_NBETA + 4]
        t1 = sm[0:nt, SM_T1:SM_T1 + 4]
        la = sm[0:nt, SM_LA:SM_LA + 4]
        g = sm[0:nt, SM_G:SM_G + 4]
        eg = sm[0:nt, SM_EG:SM_EG + 4]
        dl = sm[0:nt, SM_DL:SM_DL + 4]
        ssq = sm[0:nt, SM_SSQ:SM_SSQ + 8]
        rr = sm[0:nt, SM_RR:SM_RR + 8]
        ss2 = sm[0:nt, SM_SS2:SM_SS2 + 4]
        rstd = sm[0:nt, SM_RSTD:SM_RSTD + 4]
        beg = sm[0:nt, SM_BEG:SM_BEG + 4]
        act(beta, pb6[0:nt, hg * 4:hg * 4 + 4], AF.Sigmoid, r=[k6], w=[("sm", "beta")])
        ts(V, nbeta, beta, -1.0, ALU.mult, r=[("sm", "beta")], w=[("sm", "nbeta")])
        tt(V, t1, pb6[0:nt, 8 + hg * 4:12 + hg * 4], prm[0:nt, 8 + hg * 4:12 + hg * 4], ALU.add,
           r=[k6, "prm"], w=[("sm", "t1")])
        act(t1, t1, AF.Exp, r=[("sm", "t1")], w=[("sm", "t1")])
        act(t1, t1, AF.Ln, bias=1.0, r=[("sm", "t1")], w=[("sm", "t1")])
        tt(V, la, t1, nAe[0:nt, hs], ALU.mult, r=[("sm", "t1"), "nAe"], w=[("sm", "la")])
        MM(pb6[0:nt, 16:20], cst[0:nt, k_tri:k_tri + nt], la, r=["cst", ("sm", "la")], w=[k6])
        MM(pb6[0:nt, 20:24], cst[0:nt, k_ones:k_ones + nt], la, r=["cst", ("sm", "la")], w=[k6])
        if not sample:
            for b in range(nb):
                MM(pb6[:, 24 + 4 * b:28 + 4 * b], onesA[b * bs:(b + 1) * bs, :], sm[b * bs:(b + 1) * bs, SM_LA:SM_LA + 4],
                   r=["onesA", ("sm", "la")], w=[k6])
                act(eglb[:, b, :], pb6[:, 24 + 4 * b:28 + 4 * b], AF.Exp, r=[k6], w=[("eglb", b)])
        vcopy(g, pb6[0:nt, 16:20], r=[k6], w=[("sm", "g")])
        act(eg, pb6[0:nt, 16:20], AF.Exp, r=[k6], w=[("sm", "eg")])
        tt(V, dl, pb6[0:nt, 20:24], g, ALU.subtract, r=[k6, ("sm", "g")], w=[("sm", "dl")])
        act(dl, dl, AF.Exp, r=[("sm", "dl")], w=[("sm", "dl")])
        tt(V, beg, beta, eg, ALU.mult, r=[("sm", "beta"), ("sm", "eg")], w=[("sm", "beg")])
        if KSTOP <= 2:
            return None
        for gi in range(2):
            src = qkv[0:nt, gi, :]
            tt(V, sqs[0:nt, :], src, src, ALU.mult, r=[("qkv", gi)], w=["sqs"])
            V(lambda e, gi=gi: e.reduce_sum(out=sm[0:nt, SM_SSQ + 4 * gi:SM_SSQ + 4 * gi + 4],
                                             in_=b3(sqs)[0:nt], axis=AX.X), r=["sqs"], w=[("sm", "ssq%d" % gi)])
        ts(V, rr, ssq, RMS_EPS, ALU.add, 1.0, ALU.mult, r=[("sm", "ssq0"), ("sm", "ssq1")], w=[("sm", "rr")])
        act(rr, rr, AF.Ln, r=[("sm", "rr")], w=[("sm", "rr")])
        act(rr, rr, AF.Exp, scale=-0.5, r=[("sm", "rr")], w=[("sm", "rr")])
        ts(V, sm[0:nt, SM_RR:SM_RR + 4], sm[0:nt, SM_RR:SM_RR + 4], 128.0 ** -0.5, ALU.mult,
           r=[("sm", "rr")], w=[("sm", "rr")])
        for gi in range(2):
            src = b3(qkv[:, gi, :])[0:nt]
            tt(G, src, src, bc_last(sm[0:nt, SM_RR + 4 * gi:SM_RR + 4 * gi + 4], 128), ALU.mult,
               r=[("qkv", gi), ("sm", "rr")], w=[("qkv", gi)])
        if KSTOP <= 3:
            return None
        if KC & 1:
            pt, pk = bank()
            for h in range(4):
                TR(b3(pt)[:, h, 0:nt], qkv[0:nt, 1, h * 128:(h + 1) * 128], idn, r=[("qkv", 1), "cst"], w=[pk])
            vcopy(kT[:, :, 0:nt], b3(pt)[:, :, 0:nt], r=[pk], w=["kT"])
        if KC & 2:
            pt, pk = bank()
            for h in range(4):
                TR(b3(pt)[:, h, 0:nt], qkv[0:nt, 0, h * 128:(h + 1) * 128], idn, r=[("qkv", 0), "cst"], w=[pk])
            acopy(qT[:, :, 0:nt], b3(pt)[:, :, 0:nt], r=[pk], w=["qT"])
        if not sample and (KC & 4):
            for b in range(nb):
                vcopy(qTp[b][:, :, b * bs:(b + 1) * bs], b3(pt)[:, :, b * bs:(b + 1) * bs], r=[pk], w=["qTp%d" % b])
        if KSTOP <= 4:
            return None
        tt(V, diag[0:nt, :, 0:nt], bc_mid(idn, 4), bc_last(g, nt), ALU.mult, r=["cst", ("sm", "g")], w=["diag"])
        ptG, pkG = bank()
        MM(b3(ptG)[0:nt, :, 0:nt], onesA[0:nt, 0:nt], diag[0:nt, :, 0:nt], r=["onesA", "diag"], w=[pkG])
        dm = Dm[0:nt, :, 0:nt]
        tt(V, dm, bc_last(g, nt), b3(ptG)[0:nt, :, 0:nt], ALU.subtract, r=[("sm", "g"), pkG], w=["Dm"])
        V(lambda e: e.tensor_scalar_min(out=dm, in0=dm, scalar1=0.0), r=["Dm"], w=["Dm"])
        act(dm, dm, AF.Exp, r=["Dm"], w=["Dm"])
        tt(G, nbm[0:nt, :, 0:nt], bc_mid(cst[0:nt, k_ml:k_ml + nt], 4), bc_last(nbeta, nt), ALU.mult,
           r=["cst", ("sm", "nbeta")], w=["diag"])
        ptK, pkK = bank()
        for h in range(4):
            MM(b3(ptK)[0:nt, h, 0:nt], kT[:, h, 0:nt], kT[:, h, 0:nt], r=["kT"], w=[pkK])
        ptQ, pkQ = bank()
        for h in range(4):
            MM(b3(ptQ)[0:nt, h, 0:nt], qT[:, h, 0:nt], kT[:, h, 0:nt], r=["qT", "kT"], w=[pkQ])
        xt0 = XTb[0][0:nt, :, 0:nt]
        tt(V, xt0, b3(ptK)[0:nt, :, 0:nt], dm, ALU.mult, r=[pkK, "Dm"], w=["XT0"])
        tt(G, xt0, xt0, nbm[0:nt, :, 0:nt], ALU.mult, r=["XT0", "diag"], w=["XT0"])
        am = Am[0:nt, :, 0:nt]
        tt(V, am, b3(ptQ)[0:nt, :, 0:nt], dm, ALU.mult, r=[pkQ, "Dm"], w=["Am"])
        tt(G, am, am, bc_mid(cst[0:nt, k_mli:k_mli + nt], 4), ALU.mult, r=["Am", "cst"], w=["Am"])
        pt, pk = bank()
        for h in range(4):
            TR(b3(pt)[0:nt, h, 0:nt], XTb[0][0:nt, h, 0:nt], idn, r=["XT0", "cst"], w=[pk])
        acopy(Xb[0][0:nt, :, 0:nt], b3(pt)[0:nt, :, 0:nt], r=[pk], w=["X0"])
        tt(V, Pmb[0][0:nt, :, 0:nt], b3(pt)[0:nt, :, 0:nt], bc_mid(idn, 4), ALU.add, r=[pk, "cst"], w=["Pm0"])
        pt, pk = bank()
        for h in range(4):
            TR(b3(pt)[0:nt, h, 0:nt], Am[0:nt, h, 0:nt], idn, r=["Am", "cst"], w=[pk])
        acopy(AT[0:nt, :, 0:nt], b3(pt)[0:nt, :, 0:nt], r=[pk], w=["AT"])
        if KSTOP <= 5:
            return None
        nsteps = 3 if bs <= 8 else (4 if bs <= 16 else 6)
        p = 0
        pp = 0
        for m in range(1, nsteps):
            last = (m == nsteps - 1)
            xo, xto = Xb[p], XTb[p]
            xn, xtn = Xb[1 - p], XTb[1 - p]
            if not last:
                ptX, pkX = bank()
                for h in range(4):
                    MM(b3(ptX)[0:nt, h, 0:nt], xto[0:nt, h, 0:nt], xo[0:nt, h, 0:nt],
                       r=["XT%d" % p, "X%d" % p], w=[pkX])
            ptT, pkT = bank()
            for h in range(4):
                MM(b3(ptT)[0:nt, h, 0:nt], xo[0:nt, h, 0:nt], xto[0:nt, h, 0:nt],
                   r=["XT%d" % p, "X%d" % p], w=[pkT])
            acopy(xtn[0:nt, :, 0:nt], b3(ptT)[0:nt, :, 0:nt], r=[pkT], w=["XT%d" % (1 - p)])
            if not last:
                vcopy(xn[0:nt, :, 0:nt], b3(ptX)[0:nt, :, 0:nt], r=[pkX], w=["X%d" % (1 - p)])
            ptP, pkP = bank()
            for h in range(4):
                MM(b3(ptP)[0:nt, h, 0:nt], xtn[0:nt, h, 0:nt], Pmb[pp][0:nt, h, 0:nt],
                   r=["XT%d" % (1 - p), "Pm%d" % pp], w=[pkP])
            tt(V, Pmb[1 - pp][0:nt, :, 0:nt], b3(ptP)[0:nt, :, 0:nt], Pmb[pp][0:nt, :, 0:nt], ALU.add,
               r=[pkP, "Pm%d" % pp], w=["Pm%d" % (1 - pp)])
            p = 1 - p
            pp = 1 - pp
        TT = Pmb[pp]
        tkey = "Pm%d" % pp
        if dbg_on[0]:
            dbg("qkv", qkv[:, :, :].rearrange("p a b -> p (a b)")[:, 0:512], "qkv")
            dbg("k", qkv[:, 1, :], "qkv")
            dbg("v", qkv[:, 2, :], "qkv")
            dbg("sm", sm[:, :], "sm")
            dbg("kT", kT[:, :, :], "kT")
            dbg("Dm", Dm[:, :, :], "Dm")
            dbg("XT0", XTb[0][:, :, :], "XT0")
            dbg("X0", Xb[0][:, :, :], "X0")
            dbg("Am", Am[:, :, :], "Am")
            dbg("TT", TT[:, :, :], tkey)
        tt(G, TTb[0:nt, :, 0:nt], TT[0:nt, :, 0:nt], bc_last(beta, nt), ALU.mult, r=[tkey, ("sm", "beta")], w=["TTb"])
        tt(G, TTs[0:nt, :, 0:nt], TT[0:nt, :, 0:nt], bc_last(beg, nt), ALU.mult, r=[tkey, ("sm", "beg")], w=["TTs"])
        return dict(nt=nt, nb=nb, bs=bs, hg=hg, eg=eg, dl=dl, rstd=rstd, ss2=ss2, out_col=out_col)

    def gdn_prompt_state(info):
        nt, nb, bs, hg = info["nt"], info["nb"], info["bs"], info["hg"]
        eg, dl, rstd, ss2 = info["eg"], info["dl"], info["rstd"], info["ss2"]
        idn = ident[0:nt, 0:nt]
        pt, pk = bank()
        for h in range(4):
            MM(b3(pt)[0:nt, h, :], TTb[0:nt, h, 0:nt], qkv[0:nt, 2, h * 128:(h + 1) * 128], r=["TTb", ("qkv", 2)], w=[pk])
        acopy(U0[0:nt, :, :], b3(pt)[0:nt, :, :], r=[pk], w=["U0"])
        pt, pk = bank()
        for h in range(4):
            MM(b3(pt)[:, h, 0:nt], qkv[0:nt, 1, h * 128:(h + 1) * 128], TTs[0:nt, h, 0:nt], r=["TTs", ("qkv", 1)], w=[pk])
        vcopy(W0T[:, :, 0:nt], b3(pt)[:, :, 0:nt], r=[pk], w=["W0T"])
        for b in range(nb):
            r0, r1 = b * bs, (b + 1) * bs
            ptW, pkW = bank()
            for h in range(4):
                MM(b3(ptW)[0:nt, h, :], W0T[:, h, 0:nt], S[:, h, :], r=["W0T", "S"], w=[pkW])
            tt(V, ub[r0:r1, :, :], U0[r0:r1, :, :], b3(ptW)[r0:r1, :, :], ALU.subtract, r=["U0", pkW], w=[("u", b)])
            tt(G, ud[r0:r1, :, :], ub[r0:r1, :, :], bc_last(dl[r0:r1], 128), ALU.mult, r=[("u", b), ("sm", "dl")], w=[("ud", b)])
            ptQ, pkQ = bank()
            for h in range(4):
                MM(b3(ptQ)[0:nt, h, :], qT[:, h, 0:nt], S[:, h, :], r=["qT", "S"], w=[pkQ])
            tt(V, ob_[r0:r1, :, :], b3(ptQ)[r0:r1, :, :], bc_last(eg[r0:r1], 128), ALU.mult,
               r=[pkQ, ("sm", "eg")], w=[("o", b)])
            ptS, pkS = bank()
            for h in range(4):
                MM(b3(ptS)[:, h, :], qkv[r0:r1, 1, h * 128:(h + 1) * 128], ud[r0:r1, h, :], r=[("qkv", 1), ("ud", b)], w=[pkS])
            tt(G, S[:, :, :], S[:, :, :], bc_last(eglb[:, b, :], 128), ALU.mult, r=["S", ("eglb", b)], w=["S"])
            tt(V, S[:, :, :], S[:, :, :], b3(ptS)[:, :, :], ALU.add, r=["S", pkS], w=["S"])
        pt, pk = bank()
        for h in range(4):
            MM(b3(pt)[0:nt, h, :], AT[0:nt, h, 0:nt], ub[0:nt, h, :], r=["AT", "u"], w=[pk])
        acopy(AUs[0:nt, :, :], b3(pt)[0:nt, :, :], r=[pk], w=["Am"])
        o = ob_[0:nt, :, :]
        tt(G, o, o, AUs[0:nt, :, :], ALU.add, r=["o", "Am"], w=["o"])
        if dbg_on[0]:
            dbg("U0", U0[:, :, :], "U0")
            dbg("W0T", W0T[:, :, :], "W0T")
            dbg("u", ub[:, :, :], "u")
            dbg("S", S[:, :, :], "S")
            dbg("o", ob_[:, :, :], "o")
            dbg("eglb", eglb[:, :, :].rearrange("p a b -> p (a b)"), "eglb")
        gdn_gate(info, o)

    def gdn_sample_state(info):
        nt, hg = 128, info["hg"]
        eg, dl = info["eg"], info["dl"]
        Sall = Wg[:, :, :].rearrange("p a b -> p (a b)").bitcast(F32).rearrange("p (s h d) -> p s h d", s=16, h=4)
        UM = [cwb[i][:, :, :].rearrange("p a b -> p (a b)").rearrange("p (s d) -> p s d", s=16) for i in range(2)]
        U0T, uT, QST = Xb[0], Xb[1], XTb[0]
        for s_ in range(16):
            P.dma(dmaq(), Sall[:, s_, :, :], sdelta_d[s_, hg * 4:hg * 4 + 4].rearrange("h k v -> k h v"),
                  reads=[], writes=["Wg"])
        pt, pk = bank()
        for h in range(4):
            MM(b3(pt)[:, h, :], qkv[0:nt, 2, h * 128:(h + 1) * 128], TTb[:, h, :], r=["TTb", ("qkv", 2)], w=[pk])
        acopy(U0T[:, :, :], b3(pt)[:, :, :], r=[pk], w=["X0"])
        pt, pk = bank()
        for h in range(4):
            MM(b3(pt)[:, h, :], qkv[0:nt, 1, h * 128:(h + 1) * 128], TTs[:, h, :], r=["TTs", ("qkv", 1)], w=[pk])
        vcopy(W0T[:, :, :], b3(pt)[:, :, :], r=[pk], w=["W0T"])
        ptW, pkW = bank()
        ptQ, pkQ = bank()
        for h in range(4):
            for s_ in range(16):
                MM(b3(ptW)[:, h, s_::16], Sall[:, s_, h, :], W0T[:, h, s_::16], r=["Wg", "W0T"], w=[pkW])
                MM(b3(ptQ)[:, h, s_::16], Sall[:, s_, h, :], qT[:, h, s_::16], r=["Wg", "qT"], w=[pkQ])
        tt(V, uT[:, :, :], U0T[:, :, :], b3(ptW)[:, :, :], ALU.subtract, r=["X0", pkW], w=["X1"])
        acopy(QST[:, :, :], b3(ptQ)[:, :, :], r=[pkQ], w=["XT0"])
        pt, pk = bank()
        for h in range(4):
            MM(b3(pt)[:, h, :], uT[:, h, :], ident, r=["X1", "cst"], w=[pk])
        vcopy(ub[:, :, :], b3(pt)[:, :, :], r=[pk], w=["u"])
        tt(G, ud[:, :, :], ub[:, :, :], bc_last(dl, 128), ALU.mult, r=["u", ("sm", "dl")], w=["ud"])
        la = sm[0:128, SM_LA:SM_LA + 4]
        bm = cst[:, K_BM:K_BM + 16]
        Rt = smR.rearrange("p (s h) -> p s h", s=16)
        tt(V, Rt, la.unsqueeze(1).to_broadcast([128, 16, 4]), bm.unsqueeze(2).to_broadcast([128, 16, 4]), ALU.mult,
           r=[("sm", "la"), "cst"], w=[("tmpc0", "R")])
        MM(pb6[:, 64:128], onesA[:, :], smR, r=["onesA", ("tmpc0", "R")], w=[k6])
        act(egls, pb6[:, 64:128], AF.Exp, r=[k6], w=[("tmpc0", "E")])
        EG3 = egls.rearrange("p (s h) -> p s h", s=16)
        for h in range(4):
            um, ukey = UM[h % 2], "cwb%d" % (h % 2)
            tt(G, um, ud[:, h, :].unsqueeze(1).to_broadcast([128, 16, 128]), bm.unsqueeze(2).to_broadcast([128, 16, 128]),
               ALU.mult, r=["ud", "cst"], w=[ukey])
            for sg in range(4):
                pt, pk = bank()
                for j in range(4):
                    MM(b3(pt)[:, j, :], qkv[0:nt, 1, h * 128:(h + 1) * 128], um[:, 4 * sg + j, :], r=[("qkv", 1), ukey], w=[pk])
                sl = Sall[:, 4 * sg:4 * sg + 4, h, :]
                tt(G, sl, sl, EG3[:, 4 * sg:4 * sg + 4, h].unsqueeze(2).to_broadcast([128, 4, 128]), ALU.mult,
                   r=["Wg", ("tmpc0", "E")], w=["Wg"])
                tt(V, sl, sl, b3(pt)[:, :, :], ALU.add, r=["Wg", pk], w=["Wg"])
        for s_ in range(16):
            P.dma(dmaq(), o_sdelta[s_, hg * 4:hg * 4 + 4].rearrange("h k v -> k h v"), Sall[:, s_, :, :], reads=["Wg"])
        for h in range(4):
            MM(b3(pb7)[:, h, :], QST[:, h, :], ident, r=["XT0", "cst"], w=[k7])
        pt, pk = bank()
        for h in range(4):
            MM(b3(pt)[:, h, :], AT[:, h, :], ub[:, h, :], r=["AT", "u"], w=[pk])
        acopy(AUs[:, :, :], b3(pt)[:, :, :], r=[pk], w=["Am"])
        o = ob_[:, :, :]
        tt(V, o, b3(pb7)[:, :, :], bc_last(eg, 128), ALU.mult, r=[k7, ("sm", "eg")], w=["o"])
        tt(G, o, o, AUs[:, :, :], ALU.add, r=["o", "Am"], w=["o"])
        gdn_gate(info, o)

    def gdn_gate(info, o):
        nt, hg = info["nt"], info["hg"]
        rstd, ss2 = info["rstd"], info["ss2"]
        idn = ident[0:nt, 0:nt]
        oc = info["out_col"]
        tt(V, b3(sqs)[0:nt], o, o, ALU.mult, r=["o"], w=["sqs"])
        V(lambda e: e.reduce_sum(out=ss2, in_=b3(sqs)[0:nt], axis=AX.X), r=["sqs"], w=[("sm", "ss2")])
        ts(V, rstd, ss2, 1.0 / 128.0, ALU.mult, RMS_EPS, ALU.add, r=[("sm", "ss2")], w=[("sm", "rstd")])
        act(rstd, rstd, AF.Ln, r=[("sm", "rstd")], w=[("sm", "rstd")])
        act(rstd, rstd, AF.Exp, scale=-0.5, r=[("sm", "rstd")], w=[("sm", "rstd")])
        tt(G, o, o, bc_last(rstd, 128), ALU.mult, r=["o", ("sm", "rstd")], w=["o"])
        tt(G, o, o, bc_mid(prm[0:nt, 16:144], 4), ALU.mult, r=["o", "prm"], w=["o"])
        tt(G, o, o, b3(zs)[0:nt], ALU.mult, r=["o", "zs"], w=["o"])
        pt, pk = bank()
        for h in range(4):
            TR(b3(pt)[:, h, 0:nt], ob_[0:nt, h, :], idn, r=["o", "cst"], w=[pk])
        acopy(ogT[:, hg * 4:hg * 4 + 4, oc:oc + nt], b3(pt)[:, :, 0:nt], r=[pk], w=[("ogT", (hg, oc))])

    dbg_on = [False]
    ptiles = [(0, 16, 1, 16)] + [(16 + 128 * i, 128, 2, 64) for i in range(16)]
    kcP = (K_TRIP, K_ONESP, K_MLP, K_MLIP)
    for hg in range(2):
        for gi, c0 in enumerate((hg * 512, 1024 + hg * 512, 2048 + hg * 512, C_Z + hg * 512)):
            load_cast(Wg, "Wg", c0, 512, gi * 512)
        V(lambda e: e.memset(S[:, :, :], 0.0), r=["S"], w=["S"])
        for ti_, (t0, nt, nb, bs) in enumerate(ptiles):
            if ti_ >= KTILES:
                break
            dbg_on[0] = (KDBG and hg == 0 and ti_ == KDBG - 1)
            info = gdn_tile(hg, OFFP + t0, nt, nb, bs, kcP, None, OFFP + t0)
            if info is not None and KSTOP > 6:
                gdn_prompt_state(info)
        P.dma("sync", o_pdelta[hg * 4:hg * 4 + 4].rearrange("h k v -> k h v"), S[:, :, :], reads=["S"])
        if KTILES > 17:
            dbg_on[0] = False
            info = gdn_tile(hg, OFFS, 128, 16, 8, (K_TRIS, K_ONESS, K_MLS, K_MLIS), None, OFFS, sample=True)
            gdn_sample_state(info)

    def sample_phase():
        NEG = -1.0e30
        ALPHA = 2.0 ** 0.25
        stg = [alloc(0, "sstg", [128, 8, 256])]
        wbf = alloc(0, "wbf", [128, 8, 1024], BF16)
        identb = alloc(0, "identb", [128, 128], BF16)
        onesb = alloc(0, "onesb", [128, 128], BF16)
        OAT = alloc(0, "OAT", [128, 8, 128], BF16)
        tril = alloc(0, "tril", [128, 128])
        negm = alloc(0, "negm", [128, 128])
        wq_v = w_in.rearrange("(k p) c -> p k c", p=128)
        def vview(d):
            return d.rearrange("(k p) c -> p k c", p=128)

        def layer_norm(dst, dkey, src, skey, goff, lnb, st8):
            V(lambda e: e.reduce_sum(out=st8[:, 0:1], in_=src[:, :], axis=AX.X), r=[skey], w=[("st8", 0)])
            ts(V, st8[:, 0:1], st8[:, 0:1], 1.0 / 1024.0, ALU.mult, r=[("st8", 0)], w=[("st8", 0)])
            ts(V, src[:, :], src[:, :], st8[:, 0:1], ALU.subtract, r=[skey, ("st8", 0)], w=[skey])
            tt(V, dst[:, :], src[:, :], src[:, :], ALU.mult, r=[skey], w=[dkey])
            V(lambda e: e.reduce_sum(out=st8[:, 1:2], in_=dst[:, :], axis=AX.X), r=[dkey], w=[("st8", 1)])
            ts(V, st8[:, 1:2], st8[:, 1:2], 1.0 / 1024.0, ALU.mult, 1e-5, ALU.add, r=[("st8", 1)], w=[("st8", 1)])
            act(st8[:, 1:2], st8[:, 1:2], AF.Ln, r=[("st8", 1)], w=[("st8", 1)])
            act(st8[:, 1:2], st8[:, 1:2], AF.Exp, scale=-0.5, r=[("st8", 1)], w=[("st8", 1)])
            ts(V, dst[:, :], src[:, :], st8[:, 1:2], ALU.mult, r=[skey, ("st8", 1)], w=[dkey])
            P.dma("sync", lnb[:, :], lnp_d[:, goff:goff + 1024], writes=["lnb"])
            tt(V, dst[:, :], dst[:, :], lnb[:, :], ALU.mult, r=[dkey, "lnb"], w=[dkey])
            P.dma("sync", lnb[:, :], lnp_d[:, goff + 1024:goff + 2048], writes=["lnb"])
            tt(V, dst[:, :], dst[:, :], lnb[:, :], ALU.add, r=[dkey, "lnb"], w=[dkey])

        wq_v = vview(wq_d)
        QT = sbq("QT", [128, 8, 128], BF16)
        KTn = sbq("KTn", [128, 2, 128], BF16)
        Vn = sbq("Vn", [128, 2, 129], BF16)
        QI2 = sbq("QI2", [64, 16, 128], BF16)
        KIn = sbq("KIn", [64, 128], BF16)
        wtm = sbq("wtm", [128, 16])
        wT = sbq("wT", [16, 128], BF16)
        Asel = sbq("Asel", [16, 128], BF16)
        Wfull = sbq("Wfull", [128, 128], BF16)
        Wsel = [sbq("Wsel%d" % i, [128, 128], BF16) for i in range(2)]
        ptI = sbq("ptI", [128, 256], I32)
        ptF = sbq("ptF", [128, 256])
        IDX = sbq("IDX", [128, 256], I32)
        kip = [sbq("kip%d" % i, [128, 64]) for i in range(2)]
        kipb = [sbq("kipb%d" % i, [128, 64], BF16) for i in range(2)]
        KIT = [sbq("KIT%d" % i, [64, 2048], BF16) for i in range(2)]
        Rb = [sbq("Rb%d" % i, [128, 512], BF16) for i in range(2)]
        Iall = sbq("Iall", [128, 2056])
        Iwk = sbq("Iwk", [128, 2056])
        mx8 = sbq("mx8", [128, 8])
        maskb = sbq("maskb", [128, 2056], BF16)
        maskT = sbq("maskT", [128, 16, 128], BF16)
        mnT = sbq("mnT", [8, 128])
        MTN = sbq("MTN", [128, 128], BF16)
        Kp = [sbq("Kp%d" % i, [128, 256]) for i in range(2)]
        Vp = [sbq("Vp%d" % i, [128, 256]) for i in range(2)]
        Kpb = [sbq("Kpb%d" % i, [128, 256], BF16) for i in range(2)]
        Vpb = [sbq("Vpb%d" % i, [128, 2, 129], BF16) for i in range(2)]
        KTp = [sbq("KTp%d" % i, [128, 2, 128], BF16) for i in range(2)]
        Eb = [sbq("Eb%d" % i, [128, 64], BF16) for i in range(2)]
        PTb = [sbq("PTb%d" % i, [128, 64], BF16) for i in range(2)]
        den = sbq("den", [128, 64])

        cnt = {"stg": 0, "b": 0}

        def sbank():
            i = cnt["b"] % 3
            cnt["b"] += 1
            return pbs[i], ("pb%d" % i, None)

        def loadw(src, c0, ncol, d0=0, wkey="wbf"):
            for o_ in range(0, ncol, 256):
                n_ = min(256, ncol - o_)
                i = cnt["stg"]; cnt["stg"] += 1
                sg = stg[0]
                P.dma(dmaq(), sg[:, :, 0:n_], src[:, :, c0 + o_:c0 + o_ + n_], writes=[("sstg", 0)])
                for k in range(8):
                    (vcopy if k % 2 == 0 else acopy)(wbf[:, k, d0 + o_:d0 + o_ + n_], sg[:, k, 0:n_],
                                                     r=[("sstg", 0)], w=[(wkey, (k, d0 + o_))])

        xs = slice(OFFS, OFFS + 128)
        vcopy(identb[:, :], ident, r=["cst"], w=["identb"])
        V(lambda e: e.memset(onesb[:, :], 1.0), w=["onesb"])
        for i in range(2):
            V(lambda e, i=i: e.memset(Wsel[i][:, :], 0.0), w=["Wsel%d" % i])
            V(lambda e, i=i: e.memset(Vpb[i][:, :, 128:129], 1.0), w=[("Vpb%d" % i, "one")])
        V(lambda e: e.memset(Vn[:, :, 128:129], 1.0), w=[("Vn", "one")])
        ts(V, Asel[:, :].rearrange("p (h t) -> p h t", t=8),
           cst[0:16, K_IOTA16:K_IOTA16 + 16].unsqueeze(2).to_broadcast([16, 16, 8]),
           cst[0:16, K_PCOL:K_PCOL + 1], ALU.is_equal, r=["cst"], w=["Asel"])

        loadw(w_v, C_AQ, 1024)
        for h in range(8):
            pt, pk = sbank()
            for k in range(8):
                MM(pt[:, 0:128], wbf[:, k, h * 128:(h + 1) * 128], xTb[:, k, xs], start=(k == 0), stop=(k == 7),
                   r=["wbf", "xTb"], w=[pk])
            (vcopy if h % 2 == 0 else acopy)(QT[:, h, :], pt[:, 0:128], r=[pk], w=[("QT", h)])
        loadw(w_v, C_AK, 512)
        for kv in range(2):
            pt, pk = sbank()
            for k in range(8):
                MM(pt[:, 0:128], wbf[:, k, kv * 128:(kv + 1) * 128], xTb[:, k, xs], start=(k == 0), stop=(k == 7),
                   r=["wbf", "xTb"], w=[pk])
            vcopy(KTn[:, kv, :], pt[:, 0:128], r=[pk], w=[("KTn", kv)])
        pt, pk = sbank()
        for k in range(8):
            MM(pt[:, 0:256], xTb[:, k, xs], wbf[:, k, 256:512], start=(k == 0), stop=(k == 7), r=["wbf", "xTb"], w=[pk])
        vcopy(Vn[:, :, 0:128], pt[:, 0:256].rearrange("p (a d) -> p a d", a=2), r=[pk], w=[("Vn", "v")])
        loadw(w_v, C_IQ, 1024)
        for h in range(16):
            pt, pk = sbank()
            for k in range(8):
                MM(pt[0:64, 0:128], wbf[:, k, h * 64:(h + 1) * 64], xTb[:, k, xs], start=(k == 0), stop=(k == 7),
                   r=["wbf", "xTb"], w=[pk])
            (vcopy if h % 2 == 0 else acopy)(QI2[:, h, :], pt[0:64, 0:128], r=[pk], w=[("QI2", h)])
        loadw(w_v, C_IK, 80)
        pt, pk = sbank()
        for k in range(8):
            MM(pt[0:64, 0:128], wbf[:, k, 0:64], xTb[:, k, xs], start=(k == 0), stop=(k == 7), r=["wbf", "xTb"], w=[pk])
        vcopy(KIn[:, :], pt[0:64, 0:128], r=[pk], w=["KIn"])
        pt, pk = sbank()
        for k in range(8):
            MM(pt[:, 0:16], xTb[:, k, xs], wbf[:, k, 64:80], start=(k == 0), stop=(k == 7), r=["wbf", "xTb"], w=[pk])
        vcopy(wtm[:, :], pt[:, 0:16], r=[pk], w=["wtm"])
        pt, pk = sbank()
        MM(pt[0:16, 0:128], wtm[:, :], ident, r=["wtm", "cst"], w=[pk])
        vcopy(wT[:, :], pt[0:16, 0:128], r=[pk], w=["wT"])
        pt, pk = sbank()
        MM(pt[:, 0:128], Asel[:, :], wT[:, :], r=["Asel", "wT"], w=[pk])
        tt(V, Wfull[:, :], pt[:, 0:128], cst[:, K_MT:K_MT + 128], ALU.mult, r=[pk, "cst"], w=["Wfull"])

        P.dma("sync", ptI[:, :], ptab_d.rearrange("s j -> (s j)").partition_broadcast(128), writes=["ptI"])
        vcopy(ptF[:, :], ptI[:, :], r=["ptI"], w=["ptF"])
        ts(V, ptF[:, :], ptF[:, :], 128.0, ALU.mult, cst[:, K_PCOL:K_PCOL + 1], ALU.add, r=["ptF", "cst"], w=["ptF"])
        vcopy(IDX[:, :], ptF[:, :], r=["ptF"], w=["IDX"])

        def gather(dst, dkey, src, col):
            P.op("gpsimd", lambda e: e.indirect_dma_start(
                out=dst, out_offset=None, in_=src,
                in_offset=bass.IndirectOffsetOnAxis(ap=IDX[:, col:col + 1], axis=0)),
                reads=["IDX"], writes=[dkey], is_dma=True)

        pI = [pbs[4], pbs[5], pbs[6], pbs[7]]
        kI = [("pb4", None), ("pb5", None), ("pb6", None), ("pb7", None)]
        gi_ = 0
        for s_ in range(16):
            kit, kkey = KIT[s_ % 2], "KIT%d" % (s_ % 2)
            for j in range(16):
                kp_, kpk = kip[gi_ % 2], "kip%d" % (gi_ % 2)
                kb_, kbk = kipb[gi_ % 2], "kipb%d" % (gi_ % 2)
                gi_ += 1
                gather(kp_[:, :], kpk, cik_d[:, :], s_ * 16 + j)
                vcopy(kb_[:, :], kp_[:, :], r=[kpk], w=[kbk])
                if j % 4 == 0:
                    ptk, pkk = sbank()
                MM(ptk[0:64, (j % 4) * 128:(j % 4 + 1) * 128], kb_[:, :], identb[:, :], r=[kbk, "identb"], w=[pkk])
                if j % 4 == 3:
                    acopy(kit[:, (j - 3) * 128:(j + 1) * 128], ptk[0:64, :], r=[pkk], w=[(kkey, j // 4)])
            wi = s_ % 2
            vcopy(Wsel[wi][:, s_::16], Wfull[:, s_::16], r=["Wfull"], w=["Wsel%d" % wi])
            for c in range(4):
                pts, pks = sbank()
                MM(pts[:, :], QI2[:, :, s_::16], kit[:, c * 512:(c + 1) * 512], r=["QI2", kkey], w=[pks])
                rb, rk = Rb[c % 2], "Rb%d" % (c % 2)
                act(rb[:, :], pts[:, :], AF.Relu, scale=1.0 / 32.0, r=[pks], w=[rk])
                MM(pI[c][:, :], Wsel[wi][:, :], rb[:, :], start=(s_ == 0), stop=(s_ == 15), r=["Wsel%d" % wi, rk], w=[kI[c]])
            pts, pks = sbank()
            MM(pts[:, 0:8], QI2[:, :, s_::16], KIn[:, s_::16], r=["QI2", "KIn"], w=[pks])
            rb, rk = Rb[0], "Rb0"
            act(rb[:, 0:8], pts[:, 0:8], AF.Relu, scale=1.0 / 32.0, r=[pks], w=[rk])
            MM(pbs[3][:, 504:512], Wsel[wi][:, :], rb[:, 0:8], start=(s_ == 0), stop=(s_ == 15),
               r=["Wsel%d" % wi, rk], w=[("pb3", None)])
            V(lambda e, wi=wi, s_=s_: e.memset(Wsel[wi][:, s_::16], 0.0), r=[], w=["Wsel%d" % wi])
        for c in range(4):
            (vcopy if c % 2 == 0 else acopy)(Iall[:, c * 512:(c + 1) * 512], pI[c][:, :], r=[kI[c]], w=[("Iall", c)])
        tt(V, Iall[:, 2048:2056], pbs[3][:, 504:512], cst[:, K_MNEW:K_MNEW + 8], ALU.mult, r=[("pb3", None), "cst"], w=[("Iall", 4)])
        ts(V, mx8[:, :], cst[:, K_MNEW:K_MNEW + 8], -1.0, ALU.add, -NEG, ALU.mult, r=["cst"], w=["mx8"])
        tt(V, Iall[:, 2048:2056], Iall[:, 2048:2056], mx8[:, :], ALU.add, r=[("Iall", 4), "mx8"], w=[("Iall", 4)])
        vcopy(Iwk[:, :], Iall[:, :], r=["Iall"], w=["Iwk"])
        for r_ in range(32):
            V(lambda e: e.max(out=mx8[:, :], in_=Iwk[:, :]), r=["Iwk"], w=["mx8"])
            if r_ < 31:
                V(lambda e: e.match_replace(out=Iwk[:, :], in_to_replace=mx8[:, :], in_values=Iwk[:, :], imm_value=NEG),
                  r=["Iwk", "mx8"], w=["Iwk"])
        ts(V, maskb[:, :], Iall[:, :], mx8[:, 7:8], ALU.is_ge, r=["Iall", "mx8"], w=["maskb"])
        for j in range(16):
            if j % 4 == 0:
                ptm, pkm = sbank()
            MM(ptm[:, (j % 4) * 128:(j % 4 + 1) * 128], maskb[:, j * 128:(j + 1) * 128], identb[:, :], r=["maskb", "identb"], w=[pkm])
            if j % 4 == 3:
                (vcopy if (j // 4) % 2 == 0 else acopy)(maskT[:, j - 3:j + 1, :], ptm[:, :].rearrange("p (a q) -> p a q", a=4),
                                                        r=[pkm], w=[("maskT", j // 4)])
        pt, pk = sbank()
        MM(pt[0:8, 0:128], maskb[:, 2048:2056], identb[:, :], r=["maskb", "identb"], w=[pk])
        vcopy(mnT[:, :], pt[0:8, 0:128], r=[pk], w=["mnT"])
        pt, pk = sbank()
        MM(pt[:, 0:128], cst[0:8, K_SEL8:K_SEL8 + 128], mnT[:, :], r=["cst", "mnT"], w=[pk])
        tt(V, MTN[:, :], pt[:, 0:128], cst[:, K_ONESS:K_ONESS + 128], ALU.mult, r=[pk, "cst"], w=["MTN"])

        it = 0
        for s_ in range(16):
            pO, kO = pbs[4 + (s_ % 2) * 2], ("pb%d" % (4 + (s_ % 2) * 2), None)
            pD, kD = pbs[5 + (s_ % 2) * 2], ("pb%d" % (5 + (s_ % 2) * 2), None)
            for j in range(17):
                ib = it % 2
                it += 1
                if j < 16:
                    gather(Kp[ib][:, :], "Kp%d" % ib, ck_d[:, :], s_ * 16 + j)
                    gather(Vp[ib][:, :], "Vp%d" % ib, cv_d[:, :], s_ * 16 + j)
                    vcopy(Kpb[ib][:, :], Kp[ib][:, :], r=["Kp%d" % ib], w=["Kpb%d" % ib])
                    acopy(Vpb[ib][:, :, 0:128], Vp[ib][:, :].rearrange("p (a d) -> p a d", a=2), r=["Vp%d" % ib], w=[("Vpb%d" % ib, "v")])
                    ptt, pkt = sbank()
                    for kv in range(2):
                        MM(ptt[:, kv * 128:(kv + 1) * 128], Kpb[ib][:, kv * 128:(kv + 1) * 128], identb[:, :],
                           r=["Kpb%d" % ib, "identb"], w=[pkt])
                    vcopy(KTp[ib][:, :, :], ptt[:, 0:256].rearrange("p (a k) -> p a k", a=2), r=[pkt], w=["KTp%d" % ib])
                    kt_, ktk = KTp[ib], "KTp%d" % ib
                    vv_, vvk = Vpb[ib], ("Vpb%d" % ib, None)
                    mk_ = maskT[:, j, s_::16]
                    mkk = "maskT"
                else:
                    kt_, ktk = KTn, "KTn"
                    vv_, vvk = Vn, ("Vn", None)
                    mk_ = MTN[:, s_::16]
                    mkk = "MTN"
                pts, pks = sbank()
                for kv in range(2):
                    MM(pts[:, kv * 32:(kv + 1) * 32], kt_[:, kv, :], QT[:, kv * 4:(kv + 1) * 4, s_::16], r=[ktk, "QT"], w=[pks])
                eb, ebk = Eb[ib], "Eb%d" % ib
                act(eb[:, :], pts[:, 0:64], AF.Exp, scale=128.0 ** -0.5, r=[pks], w=[ebk])
                pb_, pbk = PTb[ib], "PTb%d" % ib
                tt(G, pb_[:, :].rearrange("p (a t) -> p a t", t=8), eb[:, :].rearrange("p (a t) -> p a t", t=8),
                   mk_.unsqueeze(1).to_broadcast([128, 8, 8]), ALU.mult, r=[ebk, mkk], w=[pbk])
                for kv in range(2):
                    MM(pO[:, kv * 32:(kv + 1) * 32], vv_[:, kv, 0:128], pb_[:, kv * 32:(kv + 1) * 32],
                       start=(j == 0), stop=(j == 16), r=[vvk, pbk], w=[kO])
                MM(pD[:, 0:64], onesb[:, :], pb_[:, :], start=(j == 0), stop=(j == 16), r=["onesb", pbk], w=[kD])
            V(lambda e, pD=pD: e.reciprocal(out=den[:, :], in_=pD[:, 0:64]), r=[kD], w=["den"])
            tt(V, OAT[:, :, s_::16], pO[:, 0:64].rearrange("p (a t) -> p a t", t=8),
               den[:, :].rearrange("p (a t) -> p a t", t=8), ALU.mult, r=[kO, "den"], w=[("OAT", s_)])

        ts(V, tril[:, :], cst[:, K_IOTA:K_IOTA + 128], cst[:, K_PCOL:K_PCOL + 1], ALU.is_le, r=["cst"], w=["tril"])
        ts(V, negm[:, :], tril[:, :], -1.0, ALU.add, -NEG, ALU.mult, r=["tril"], w=["negm"])

        def finish_tile(xcols, oat_t, xtm_src, y_dst):
            phase_barrier(2)
            hT = alloc(2, "hT", [128, 8, 128], BF16)
            htm = alloc(2, "htm", [128, 1024])
            CT = alloc(2, "CT", [128, 128, 128], BF16)
            T4 = [sbq("T4_%d" % i, [128, 8, 128]) for i in range(4)]
            mT = sbq("mT", [128, 8, 128], BF16)
            xtm = sbq("xtm", [128, 1024])
            lnb = sbq("lnb", [128, 1024])
            hpre = sbq("hpre", [128, 1024])
            st8 = sbq("st8", [128, 8])

            srcs = [(vview(wbg_d), 0, ogT, "ogT", xcols), (vview(wba_d), 0, oat_t, "OAT", slice(0, 128)),
                    (w_v, C_GA, xTb, "xTb", xcols), (w_v, C_GB, xTb, "xTb", xcols)]
            for wi_, (src, c0, act_t, akey, cols) in enumerate(srcs):
                loadw(src, c0, 1024)
                for j in range(8):
                    pt, pk = sbank()
                    for k in range(8):
                        MM(pt[:, 0:128], wbf[:, k, j * 128:(j + 1) * 128], act_t[:, k, cols], start=(k == 0), stop=(k == 7),
                           r=["wbf", akey], w=[pk])
                    if wi_ < 2:
                        (vcopy if j % 2 == 0 else acopy)(T4[wi_][:, j, :], pt[:, 0:128], r=[pk], w=[("T4_%d" % wi_, j)])
                    else:
                        act(T4[wi_][:, j, :], pt[:, 0:128], AF.Sigmoid, r=[pk], w=[("T4_%d" % wi_, j)])
            tt(V, T4[0][:, :, :], T4[0][:, :, :], T4[2][:, :, :], ALU.mult, r=["T4_0", "T4_2"], w=["T4_0"])
            tt(G, T4[1][:, :, :], T4[1][:, :, :], T4[3][:, :, :], ALU.mult, r=["T4_1", "T4_3"], w=["T4_1"])
            tt(V, mT[:, :, :], T4[0][:, :, :], T4[1][:, :, :], ALU.add, r=["T4_0", "T4_1"], w=["mT"])

            P.dma("sync", xtm[:, :], xtm_src, writes=["xtm"])
            for half in range(2):
                loadw(vview(wout_d), half * 512, 512)
                pt, pk = sbank()
                for k in range(8):
                    MM(pt[:, :], mT[:, k, :], wbf[:, k, 0:512], start=(k == 0), stop=(k == 7), r=["mT", "wbf"], w=[pk])
                V(lambda e, pt=pt, half=half: e.scalar_tensor_tensor(out=hpre[:, half * 512:(half + 1) * 512], in0=xtm[:, half * 512:(half + 1) * 512],
                                                                     scalar=ALPHA, in1=pt[:, :], op0=ALU.mult, op1=ALU.add),
                  r=["xtm", pk], w=[("hpre", half)])
            layer_norm(htm, "htm", hpre, "hpre", 0, lnb, st8)
            for k in range(8):
                if k % 4 == 0:
                    pt, pk = sbank()
                MM(pt[:, (k % 4) * 128:(k % 4 + 1) * 128], htm[:, k * 128:(k + 1) * 128], ident, r=["htm", "cst"], w=[pk])
                if k % 4 == 3:
                    vcopy(hT[:, k - 3:k + 1, :], pt[:, :].rearrange("p (a q) -> p a q", a=4), r=[pk], w=[("hT", k // 4)])
            phase_barrier(3)
            SV = alloc(2, "SV", [128, 16, 16])
            SIu = alloc(2, "SIu", [128, 16, 16], U32)
            SIf = alloc(2, "SIf", [128, 16, 16])
            qryT = sbq("qryT", [128, 16, 128], BF16)
            skst = sbq("skst", [128, 4, 128])
            skb = sbq("skb", [128, 16, 128], BF16)
            SC = sbq("SC", [128, 16, 128])
            wk = sbq("wk", [128, 256])
            for half in range(2):
                loadw(wq_v, half * 1024, 1024)
                for j in range(8):
                    pt, pk = sbank()
                    for k in range(8):
                        MM(pt[:, 0:128], wbf[:, k, j * 128:(j + 1) * 128], hT[:, k, :], start=(k == 0), stop=(k == 7),
                           r=["wbf", "hT"], w=[pk])
                    (vcopy if j % 2 == 0 else acopy)(qryT[:, half * 8 + j, :], pt[:, 0:128], r=[pk], w=[("qryT", half * 8 + j)])
            for q4 in range(4):
                P.dma(dmaq(), skst[:, :, :], skT_d[:, q4 * 4:(q4 + 1) * 4, :], writes=["skst"])
                vcopy(skb[:, q4 * 4:(q4 + 1) * 4, :], skst[:, :, :], r=["skst"], w=[("skb", q4)])
            for hc in range(16):
                if hc % 4 == 0:
                    pt, pk = sbank()
                MM(pt[:, (hc % 4) * 128:(hc % 4 + 1) * 128], qryT[:, hc, :], skb[:, hc, :], r=["qryT", "skb"], w=[pk])
                if hc % 4 == 3:
                    (vcopy if (hc // 4) % 2 == 0 else acopy)(SC[:, hc - 3:hc + 1, :], pt[:, :].rearrange("p (a q) -> p a q", a=4),
                                                             r=[pk], w=[("SC", hc // 4)])
            for hc in range(16):
                V(lambda e, hc=hc: e.max(out=SV[:, hc, 0:8], in_=SC[:, hc, :]), r=["SC"], w=[("SV", hc)])
                V(lambda e, hc=hc: e.max_index(out=SIu[:, hc, 0:8], in_max=SV[:, hc, 0:8], in_values=SC[:, hc, :]),
                  r=["SC", ("SV", hc)], w=[("SIu", hc)])
                V(lambda e, hc=hc: e.match_replace(out=wk[:, 0:128], in_to_replace=SV[:, hc, 0:8], in_values=SC[:, hc, :], imm_value=NEG),
                  r=["SC", ("SV", hc)], w=["wk"])
                V(lambda e, hc=hc: e.max(out=SV[:, hc, 8:16], in_=wk[:, 0:128]), r=["wk"], w=[("SV", hc)])
                V(lambda e, hc=hc: e.max_index(out=SIu[:, hc, 8:16], in_max=SV[:, hc, 8:16], in_values=wk[:, 0:128]),
                  r=["wk", ("SV", hc)], w=[("SIu", hc)])
            vcopy(SIf[:, :, :], SIu[:, :, :], r=["SIu"], w=["SIf"])
            phase_barrier(3)
            cand = sbq("cand", [128, 8, 256])
            Cc = sbq("Cc", [128, 8, 256])
            c8 = sbq("c8", [128, 16])
            tau = sbq("tau", [128, 8])
            cmx = sbq("cmx", [128, 8])
            zz = sbq("zz", [128, 8])
            cT = sbq("cT", [128, 16, 128])
            I0T = sbq("I0T", [128, 128])
            I1T = sbq("I1T", [128, 128])
            P1x = [sbq("P1x%d" % i, [128, 128]) for i in range(2)]
            P1 = [sbq("P1_%d" % i, [128, 128], BF16) for i in range(2)]
            A1 = [sbq("A1_%d" % i, [128, 128], BF16) for i in range(2)]
            cblk = [sbq("cblk%d" % i, [128, 128], BF16) for i in range(2)]
            Rr = [sbq("Rr%d" % i, [128, 128], BF16) for i in range(2)]

            wk = sbq("wk2", [128, 256])
            SV4 = SV[:, :, :].rearrange("p (h c) a -> p h c a", c=2)
            cand4 = cand[:, :, :].rearrange("p h (a b) -> p h a b", a=16)
            tt(V, cand4, SV4[:, :, 0, :].unsqueeze(3).to_broadcast([128, 8, 16, 16]),
               SV4[:, :, 1, :].unsqueeze(2).to_broadcast([128, 8, 16, 16]), ALU.add, r=["SV"], w=["cand"])
            for h in range(8):
                V(lambda e, h=h: e.max(out=c8[:, 0:8], in_=cand[:, h, :]), r=["cand"], w=["c8"])
                vcopy(cmx[:, h:h + 1], c8[:, 0:1], r=["c8"], w=[("cmx", h)])
                V(lambda e, h=h: e.match_replace(out=wk[:, :], in_to_replace=c8[:, 0:8], in_values=cand[:, h, :], imm_value=NEG),
                  r=["cand", "c8"], w=["wk"])
                V(lambda e: e.max(out=c8[:, 8:16], in_=wk[:, :]), r=["wk"], w=["c8"])
                vcopy(tau[:, h:h + 1], c8[:, 15:16], r=["c8"], w=[("tau", h)])
            tt(V, Cc[:, :, :], cand[:, :, :], cmx[:, :].unsqueeze(2).to_broadcast([128, 8, 256]), ALU.subtract, r=["cand", "cmx"], w=["Cc"])
            act(Cc[:, :, :], Cc[:, :, :], AF.Exp, r=["Cc"], w=["Cc"])
            tt(V, cand[:, :, :], cand[:, :, :], tau[:, :].unsqueeze(2).to_broadcast([128, 8, 256]), ALU.is_ge, r=["cand", "tau"], w=["cand"])
            tt(V, Cc[:, :, :], Cc[:, :, :], cand[:, :, :], ALU.mult, r=["Cc", "cand"], w=["Cc"])
            V(lambda e: e.reduce_sum(out=zz[:, :], in_=Cc[:, :, :], axis=AX.X), r=["Cc"], w=["zz"])
            V(lambda e: e.reciprocal(out=zz[:, :], in_=zz[:, :]), r=["zz"], w=["zz"])
            tt(V, Cc[:, :, :], Cc[:, :, :], zz[:, :].unsqueeze(2).to_broadcast([128, 8, 256]), ALU.mult, r=["Cc", "zz"], w=["Cc"])
            Cc4 = Cc[:, :, :].rearrange("p h (a b) -> p h a b", a=16)
            for a in range(16):
                if a % 4 == 0:
                    pt, pk = sbank()
                ctmp = P1x[a % 2]
                (vcopy if a % 2 == 0 else acopy)(ctmp[:, :].rearrange("p (h b) -> p h b", h=8), Cc4[:, :, a, :], r=["Cc"], w=["P1x%d" % (a % 2)])
                MM(pt[:, (a % 4) * 128:(a % 4 + 1) * 128], ctmp[:, :], ident, r=["P1x%d" % (a % 2), "cst"], w=[pk])
                if a % 4 == 3:
                    (vcopy if (a // 4) % 2 == 0 else acopy)(cT[:, a - 3:a + 1, :], pt[:, :].rearrange("p (a q) -> p a q", a=4),
                                                            r=[pk], w=[("cT", a // 4)])
            SI4 = SIf[:, :, :].rearrange("p (h c) a -> p h c a", c=2)
            pt, pk = sbank()
            for c_ in range(2):
                vcopy(P1x[c_][:, :].rearrange("p (h b) -> p h b", h=8), SI4[:, :, c_, :], r=["SIf"], w=["P1x%d" % c_])
                MM(pt[:, c_ * 128:(c_ + 1) * 128], P1x[c_][:, :], ident, r=["P1x%d" % c_, "cst"], w=[pk])
            vcopy(I0T[:, :], pt[:, 0:128], r=[pk], w=["I0T"])
            vcopy(I1T[:, :], pt[:, 128:256], r=[pk], w=["I1T"])
            iota = cst[:, K_IOTA:K_IOTA + 128]
            for t in range(128):
                i2 = t % 2
                ts(V, P1[i2][:, :], iota, I1T[:, t:t + 1], ALU.is_equal, r=["cst", "I1T"], w=["P1_%d" % i2])
                ts(V, A1[i2][:, :], iota, I0T[:, t:t + 1], ALU.is_equal, r=["cst", "I0T"], w=["A1_%d" % i2])
                tt(G, cblk[i2][:, :].rearrange("p (h a) -> p h a", h=8), cT[:, :, t].unsqueeze(1).to_broadcast([128, 8, 16]),
                   cst[:, K_BD:K_BD + 8].unsqueeze(2).to_broadcast([128, 8, 16]), ALU.mult, r=["cT", "cst"], w=["cblk%d" % i2])
                pt, pk = sbank()
                MM(pt[:, 0:128], cblk[i2][:, :], P1[i2][:, :], r=["cblk%d" % i2, "P1_%d" % i2], w=[pk])
                acopy(Rr[i2][:, :], pt[:, 0:128], r=[pk], w=["Rr%d" % i2])
                pt2, pk2 = sbank()
                MM(pt2[:, 0:128], Rr[i2][:, :], A1[i2][:, :], r=["Rr%d" % i2, "A1_%d" % i2], w=[pk2])
                acopy(CT[:, :, t], pt2[:, 0:128], r=[pk2], w=[("CT", t)])
            phase_barrier(3)
            ust = [sbq("ust%d" % i, [128, 8, 128]) for i in range(2)]
            vst = [sbq("vst%d" % i, [128, 1024]) for i in range(2)]
            ub16 = [sbq("ub16_%d" % i, [128, 8, 128], BF16) for i in range(2)]
            vb16 = [sbq("vb16_%d" % i, [128, 1024], BF16) for i in range(2)]
            gg = [sbq("gg%d" % i, [128, 128]) for i in range(2)]
            lw = [sbq("lw%d" % i, [128, 128], BF16) for i in range(2)]
            ypre = sbq("ypre", [128, 1024])
            yout = sbq("yout", [128, 1024])
            lnb2 = sbq("lnb2", [128, 1024])
            st8b = sbq("st8b", [128, 8])
            pA0, kA0 = pbs[4], ("pb4", None)
            pA1, kA1 = pbs[5], ("pb5", None)
            for i in range(128):
                i2 = i % 2
                P.dma("sync", ust[i2][:, :, :], uT_d[i].rearrange("(k p) j -> p k j", p=128), writes=["ust%d" % i2])
                P.dma("sync", vst[i2][:, :], pv_d[i * 128:(i + 1) * 128, :], writes=["vst%d" % i2])
                vcopy(ub16[i2][:, :, :], ust[i2][:, :, :], r=["ust%d" % i2], w=["ub16_%d" % i2])
                acopy(vb16[i2][:, :], vst[i2][:, :], r=["vst%d" % i2], w=["vb16_%d" % i2])
                pt, pk = sbank()
                for k in range(8):
                    MM(pt[:, 0:128], ub16[i2][:, k, :], hT[:, k, :], start=(k == 0), stop=(k == 7), r=["ub16_%d" % i2, "hT"], w=[pk])
                act(gg[i2][:, :], pt[:, 0:128], AF.Gelu, r=[pk], w=["gg%d" % i2])
                tt(G, lw[i2][:, :], gg[i2][:, :], CT[:, i, :], ALU.mult, r=["gg%d" % i2, "CT"], w=["lw%d" % i2])
                MM(pA0[:, :], lw[i2][:, :], vb16[i2][:, 0:512], start=(i == 0), stop=(i == 127), r=["lw%d" % i2, "vb16_%d" % i2], w=[kA0])
                MM(pA1[:, :], lw[i2][:, :], vb16[i2][:, 512:1024], start=(i == 0), stop=(i == 127), r=["lw%d" % i2, "vb16_%d" % i2], w=[kA1])
            for half, (pa, ka) in enumerate(((pA0, kA0), (pA1, kA1))):
                V(lambda e, pa=pa, half=half: e.scalar_tensor_tensor(out=ypre[:, half * 512:(half + 1) * 512], in0=htm[:, half * 512:(half + 1) * 512],
                                                                     scalar=ALPHA, in1=pa[:, :], op0=ALU.mult, op1=ALU.add),
                  r=["htm", ka], w=[("ypre", half)])
            layer_norm(yout, "yout", ypre, "ypre", 2048, lnb2, st8b)
            P.dma("sync", y_dst, yout[:, :], reads=["yout"])

            phase_barrier(2)

        finish_tile(xs, OAT, xstm_d[:, :], o_ys[:, :])

        phase_barrier(1)
        KTa = alloc(1, "KTa", [128, 2, 2176], BF16)
        Va = alloc(1, "Va", [128, 17, 2, 129], BF16)
        KIa = alloc(1, "KIa", [64, 2176], BF16)
        V(lambda e: e.memset(Va[:, :, :, 128:129], 1.0), w=[("Va", "one")])
        loadw(w_v, C_AK, 512)
        for tb in range(17):
            cols = slice(OFFP + 128 * tb, OFFP + 128 * tb + 128)
            for kv in range(2):
                pt, pk = sbank()
                for k in range(8):
                    MM(pt[:, 0:128], wbf[:, k, kv * 128:(kv + 1) * 128], xTb[:, k, cols], start=(k == 0), stop=(k == 7),
                       r=["wbf", "xTb"], w=[pk])
                (vcopy if kv == 0 else acopy)(KTa[:, kv, tb * 128:(tb + 1) * 128], pt[:, 0:128], r=[pk], w=[("KTa", (kv, tb))])
            pt, pk = sbank()
            for k in range(8):
                MM(pt[:, 0:256], xTb[:, k, cols], wbf[:, k, 256:512], start=(k == 0), stop=(k == 7), r=["wbf", "xTb"], w=[pk])
            vcopy(Va[:, tb, :, 0:128], pt[:, 0:256].rearrange("p (a d) -> p a d", a=2), r=[pk], w=[("Va", tb)])
        loadw(w_v, C_IK, 64)
        for tb in range(17):
            cols = slice(OFFP + 128 * tb, OFFP + 128 * tb + 128)
            pt, pk = sbank()
            for k in range(8):
                MM(pt[0:64, 0:128], wbf[:, k, 0:64], xTb[:, k, cols], start=(k == 0), stop=(k == 7), r=["wbf", "xTb"], w=[pk])
            acopy(KIa[:, tb * 128:(tb + 1) * 128], pt[0:64, 0:128], r=[pk], w=[("KIa", tb)])

        def prompt_block(bq):
            nkc = bq + 1
            nk = 128 * nkc
            cols = slice(OFFP + 128 * bq, OFFP + 128 * bq + 128)
            QT = sbq("pQT", [128, 8, 128], BF16)
            QI2 = sbq("pQI2", [64, 16, 128], BF16)
            wtm = sbq("pwtm", [128, 16])
            Iall = sbq("pIall", [128, 2176])
            Iwk = sbq("pIwk", [128, 2176])
            tmpI = [sbq("ptmpI%d" % i, [128, 512]) for i in range(2)]
            mx8 = sbq("pmx8", [128, 8])
            thr = sbq("pthr", [128, 1])
            maskb = sbq("pmaskb", [128, 2176], BF16)
            maskT = sbq("pmaskT", [128, 17, 128], BF16)
            Eb = [sbq("pEb%d" % i, [128, 512], BF16) for i in range(2)]
            PTb = [sbq("pPTb%d" % i, [128, 512], BF16) for i in range(2)]
            rec = sbq("prec", [128, 512])
            loadw(w_v, C_AQ, 1024)
            for h in range(8):
                pt, pk = sbank()
                for k in range(8):
                    MM(pt[:, 0:128], wbf[:, k, h * 128:(h + 1) * 128], xTb[:, k, cols], start=(k == 0), stop=(k == 7),
                       r=["wbf", "xTb"], w=[pk])
                (vcopy if h % 2 == 0 else acopy)(QT[:, h, :], pt[:, 0:128], r=[pk], w=[("pQT", h)])
            loadw(w_v, C_IQ, 1024)
            for h in range(16):
                pt, pk = sbank()
                for k in range(8):
                    MM(pt[0:64, 0:128], wbf[:, k, h * 64:(h + 1) * 64], xTb[:, k, cols], start=(k == 0), stop=(k == 7),
                       r=["wbf", "xTb"], w=[pk])
                (vcopy if h % 2 == 0 else acopy)(QI2[:, h, :], pt[0:64, 0:128], r=[pk], w=[("pQI2", h)])
            loadw(w_v, C_IW, 16)
            pt, pk = sbank()
            for k in range(8):
                MM(pt[:, 0:16], xTb[:, k, cols], wbf[:, k, 0:16], start=(k == 0), stop=(k == 7), r=["wbf", "xTb"], w=[pk])
            ts(V, wtm[:, :], pt[:, 0:16], 1.0 / 32.0, ALU.mult, r=[pk], w=["pwtm"])
            for c in range((nk + 511) // 512):
                wd = min(512, nk - 512 * c)
                ksl = slice(512 * c, 512 * c + wd)
                for h in range(16):
                    pts, pks = sbank()
                    MM(pts[:, 0:wd], QI2[:, h, :], KIa[:, ksl], r=["pQI2", "KIa"], w=[pks])
                    if h == 0:
                        ts(V, Iall[:, ksl], pts[:, 0:wd], 0.0, ALU.max, wtm[:, 0:1], ALU.mult, r=[pks, "pwtm"], w=[("pIall", c)])
                    else:
                        tm_, tk_ = tmpI[h % 2], "ptmpI%d" % (h % 2)
                        ts(V, tm_[:, 0:wd], pts[:, 0:wd], 0.0, ALU.max, wtm[:, h:h + 1], ALU.mult, r=[pks, "pwtm"], w=[tk_])
                        tt(G, Iall[:, ksl], Iall[:, ksl], tm_[:, 0:wd], ALU.add, r=[("pIall", c), tk_], w=[("pIall", c)])
            dsl = slice(128 * bq, 128 * bq + 128)
            tt(V, Iall[:, dsl], Iall[:, dsl], tril[:, :], ALU.mult, r=["pIall", "tril"], w=["pIall"])
            tt(V, Iall[:, dsl], Iall[:, dsl], negm[:, :], ALU.add, r=["pIall", "negm"], w=["pIall"])
            vcopy(Iwk[:, 0:nk], Iall[:, 0:nk], r=["pIall"], w=["pIwk"])
            nr = min(32, nk // 8)
            for r_ in range(nr):
                V(lambda e: e.max(out=mx8[:, :], in_=Iwk[:, 0:nk]), r=["pIwk"], w=["pmx8"])
                if r_ < nr - 1:
                    V(lambda e: e.match_replace(out=Iwk[:, 0:nk], in_to_replace=mx8[:, :], in_values=Iwk[:, 0:nk], imm_value=NEG),
                      r=["pIwk", "pmx8"], w=["pIwk"])
            ts(V, thr[:, :], mx8[:, 7:8], -1.0e29, ALU.max, r=["pmx8"], w=["pthr"])
            ts(V, maskb[:, 0:nk], Iall[:, 0:nk], thr[:, 0:1], ALU.is_ge, r=["pIall", "pthr"], w=["pmaskb"])
            for j in range(nkc):
                if j % 4 == 0:
                    ptm, pkm = sbank()
                    j0 = j
                MM(ptm[:, (j % 4) * 128:(j % 4 + 1) * 128], maskb[:, j * 128:(j + 1) * 128], identb[:, :], r=["pmaskb", "identb"], w=[pkm])
                if j % 4 == 3 or j == nkc - 1:
                    n_ = j - j0 + 1
                    (vcopy if (j // 4) % 2 == 0 else acopy)(maskT[:, j0:j + 1, :], ptm[:, 0:128 * n_].rearrange("p (a q) -> p a q", a=n_),
                                                            r=[pkm], w=[("pmaskT", j // 4)])
            it = 0
            for kv in range(2):
                pO, kO = pbs[4 + 2 * kv], ("pb%d" % (4 + 2 * kv), None)
                pD, kD = pbs[5 + 2 * kv], ("pb%d" % (5 + 2 * kv), None)
                qmov = QT[:, 4 * kv:4 * kv + 4, :].rearrange("p a q -> p (a q)")
                for j in range(nkc):
                    ib = it % 2
                    it += 1
                    pts, pks = sbank()
                    MM(pts[:, :], KTa[:, kv, j * 128:(j + 1) * 128], qmov, r=["KTa", "pQT"], w=[pks])
                    eb, ebk = Eb[ib], "pEb%d" % ib
                    act(eb[:, :], pts[:, :], AF.Exp, scale=128.0 ** -0.5, r=[pks], w=[ebk])
                    pb_, pbk = PTb[ib], "pPTb%d" % ib
                    tt(G, pb_[:, :].rearrange("p (a q) -> p a q", a=4), eb[:, :].rearrange("p (a q) -> p a q", a=4),
                       maskT[:, j, :].unsqueeze(1).to_broadcast([128, 4, 128]), ALU.mult, r=[ebk, "pmaskT"], w=[pbk])
                    MM(pO[:, :], Va[:, j, kv, 0:128], pb_[:, :], start=(j == 0), stop=(j == nkc - 1), r=["Va", pbk], w=[kO])
                    MM(pD[:, :], onesb[:, :], pb_[:, :], start=(j == 0), stop=(j == nkc - 1), r=["onesb", pbk], w=[kD])
                V(lambda e, pD=pD: e.reciprocal(out=rec[:, :], in_=pD[:, :]), r=[kD], w=["prec"])
                tt(V, OAT[:, 4 * kv:4 * kv + 4, :].rearrange("p a q -> p (a q)"), pO[:, :], rec[:, :], ALU.mult,
                   r=[kO, "prec"], w=[("OAT", kv)])

        for bq in range(NPB):
            phase_barrier(2)
            prompt_block(bq)
            finish_tile(slice(OFFP + 128 * bq, OFFP + 128 * bq + 128), OAT, xptm_d[128 * bq:128 * bq + 128, :],
                        o_yp[128 * bq:128 * bq + 128, :])

    def phase_barrier(level=0):
        P.barrier({
            "sync": lambda e: e.nop(),
            "gpsimd": lambda e: e.memset(dummy[:, 1:2], 0.0),
            "vector": lambda e: e.memset(dummy[:, 0:1], 0.0),
            "scalar": lambda e: e.copy(out=dummy[:, 2:3], in_=dummy[:, 3:4]),
            "tensor": lambda e: e.matmul(pbs[6][0:1, 500:501], lhsT=onesA[0:1, 0:1], rhs=onesA[0:1, 0:1], start=True, stop=True),
        })
        for lv in range(3, level - 1, -1):
            stk[lv].close()
            stk[lv] = contextlib.ExitStack()
        if level == 0:
            es_ph[0].close()
            es_ph[0] = contextlib.ExitStack()

    phase_barrier()
    sample_phase()

    P.emit(es)
    for lv in range(3, -1, -1):
        stk[lv].close()
    es_ph[0].close()
    es.close()
    nc._dbg_names = dbg_names
    return nc


_NC_CACHE = {}


def kernel(x_prompt, x_sample, cache_k, cache_v, cache_idx_k, state_conv, state_delta, page_table,
           meta_tokens, w_in, conv_w, a_log, dt_bias, gdn_norm_g, w_branch_gdn, w_branch_attn, w_out,
           ln1_g, ln1_b, peer_wq, peer_subkeys, peer_u, peer_v, ln2_g, ln2_b):
    f = np.float32
    x_prompt = np.asarray(x_prompt, f)
    x_sample = np.asarray(x_sample, f)
    meta = np.asarray(meta_tokens, f)
    w_in0 = np.ascontiguousarray(np.asarray(w_in, f)[0])
    state_conv = np.asarray(state_conv, f)
    state_delta = np.asarray(state_delta, f)
    page_table = np.asarray(page_table, np.int32)
    ck2 = np.asarray(cache_k, f)[0].reshape(2560 * 128, 256)
    cv2 = np.asarray(cache_v, f)[0].reshape(2560 * 128, 256)
    cik2 = np.asarray(cache_idx_k, f)[0].reshape(2560 * 128, 64)
    wbg0 = np.ascontiguousarray(np.asarray(w_branch_gdn, f)[0])
    wba0 = np.ascontiguousarray(np.asarray(w_branch_attn, f)[0])
    wout0 = np.ascontiguousarray(np.asarray(w_out, f)[0])
    lnp = np.concatenate([np.asarray(a, f)[0] for a in (ln1_g, ln1_b, ln2_g, ln2_b)])
    lnp = np.ascontiguousarray(np.broadcast_to(lnp[None, :], (128, 4096)))
    wq0 = np.ascontiguousarray(np.asarray(peer_wq, f)[0])
    skT = np.ascontiguousarray(np.asarray(peer_subkeys, f)[0].reshape(16, 128, 128).transpose(2, 0, 1))
    uT = np.ascontiguousarray(np.asarray(peer_u, f)[0].reshape(128, 128, D).transpose(0, 2, 1))
    pv0 = np.ascontiguousarray(np.asarray(peer_v, f)[0])
    conv_w0 = np.ascontiguousarray(np.asarray(conv_w, f)[0])
    params = np.concatenate([np.asarray(a_log, f)[0], np.asarray(dt_bias, f)[0], np.asarray(gdn_norm_g, f)[0]])
    params = np.ascontiguousarray(np.broadcast_to(params[None, :], (128, 144)))
    consts = make_consts()
    if "nc" not in _NC_CACHE:
        _NC_CACHE["nc"] = build()
    nc = _NC_CACHE["nc"]
    in_maps = []
    for c in range(NCORES):
        xT = np.zeros((D, NTC), f)
        xT[:, OFFP:OFFP + NMETA] = meta.T
        xT[:, OFFP + NMETA:OFFP + LP] = x_prompt[c].T
        xs = x_sample[16 * c:16 * (c + 1)]
        xT[:, OFFS:OFFS + NS] = xs.transpose(1, 0, 2).reshape(NS, D).T
        in_maps.append({"xT": xT, "w_in": w_in0, "conv_w": conv_w0, "consts": consts, "params": params,
                        "state_conv": np.ascontiguousarray(state_conv[0, 16 * c:16 * c + 16]),
                        "state_delta": np.ascontiguousarray(state_delta[0, 16 * c:16 * c + 16]),
                        "page_table": np.ascontiguousarray(page_table[16 * c:16 * c + 16]),
                        "cache_k": ck2, "cache_v": cv2, "cache_ik": cik2,
                        "w_bg": wbg0, "w_ba": wba0, "w_out": wout0, "lnp": lnp,
                        "xs_tm": np.ascontiguousarray(xs.transpose(1, 0, 2).reshape(NS, D)),
                        "peer_wq": wq0, "skT": skT, "peer_uT": uT, "peer_v": pv0,
                        "xp_tm": np.ascontiguousarray(np.concatenate([meta, x_prompt[c], np.zeros((112, D), f)], axis=0))})
    res = run_bass_kernel_spmd(nc, in_maps, core_ids=list(range(NCORES)))
    R = res.results

    def cat(name):
        return np.stack([np.asarray(R[c][name]) for c in range(NCORES)], axis=0)

    def samp(name, width):
        a = cat(name).reshape(NCORES, 8, 16, width).transpose(0, 2, 1, 3)
        return np.ascontiguousarray(a).reshape(128, 8, width)

    p_k = cat("p_k").reshape(1, 8, LP, 2, 128)
    p_v = cat("p_v").reshape(1, 8, LP, 2, 128)
    p_ik = cat("p_idx_k").reshape(1, 8, LP, 64)
    s_k = samp("s_k", 256).reshape(1, 128, 8, 2, 128)
    s_v = samp("s_v", 256).reshape(1, 128, 8, 2, 128)
    s_ik = samp("s_idx_k", 64).reshape(1, 128, 8, 64)
    p_conv = cat("p_conv").reshape(1, 8, 3, 3072)
    s_conv = np.ascontiguousarray(samp("s_conv", 3072)[:, 5:8]).reshape(1, 128, 3, 3072)
    p_delta = cat("p_delta").reshape(1, 8, 8, 128, 128)
    if "dbg" in R[0]:
        _NC_CACHE["dbg"] = (np.asarray(R[0]["dbg"]), list(nc._dbg_names))
    y_prompt = np.ascontiguousarray(cat("y_prompt")[:, NMETA:LP, :])
    y_sample = samp("y_sample", D)
    s_delta = cat("s_delta").reshape(1, 128, 8, 128, 128)
    return (y_prompt, y_sample, p_conv, p_delta, p_k, p_v, p_ik, s_conv, s_delta, s_k, s_v, s_ik)
```
